# Optimizing a Trainium2 kernel written in Bass

```python
import math
import jax, jax.numpy as jnp
from jax import lax
import numpy as np

D_MODEL = 1024
BATCH = 8
SEQ = 2048
DEPTH = 1

SSD_EXPAND = 2
SSD_D_INNER = SSD_EXPAND * D_MODEL
SSD_HEAD_DIM = 64
SSD_HEADS = SSD_D_INNER // SSD_HEAD_DIM
SSD_GROUPS = 4
SSD_HEADS_PER_GROUP = SSD_HEADS // SSD_GROUPS
SSD_STATE = 128
SSD_CONV = 4
SSD_CHUNK = 128
SSD_CONV_CH = SSD_D_INNER + 2 * SSD_GROUPS * SSD_STATE
SSD_DT_MIN = 0.001
SSD_DT_MAX = 0.1

DA_HEAD_DIM = 64
DA_HEADS = D_MODEL // (2 * DA_HEAD_DIM)
DA_V_DIM = 2 * DA_HEAD_DIM
DA_WIDTH = DA_HEADS * 2 * DA_HEAD_DIM
Q_BLOCK = 128

N_BRANCH = 2
D_FF = 2816
EPS = 1e-6

IN_SIZES = [SSD_D_INNER, SSD_CONV_CH, SSD_HEADS, DA_WIDTH, DA_WIDTH, DA_WIDTH, N_BRANCH * D_MODEL]
D_IN_PROJ = sum(IN_SIZES)
IN_SPLITS = np.cumsum(IN_SIZES)[:-1].tolist()

kernel_name = "hybrid_ssd_diffattn_macaron_block"


def rmsnorm(x, g):
    xf = x.astype(jnp.float32)
    y = xf * lax.rsqrt(jnp.mean(xf * xf, axis=-1, keepdims=True) + EPS)
    return y.astype(x.dtype) * g


def swiglu(x, w_gate, w_up, w_down):
    return (jax.nn.silu(x @ w_gate) * (x @ w_up)) @ w_down


def alibi_slopes(n_heads):
    return jnp.asarray(2.0 ** (-8.0 * np.arange(1, n_heads + 1) / n_heads), dtype=jnp.float32)


def causal_dwconv(x, w, b):
    k = w.shape[0]
    y = lax.conv_general_dilated(x, w[:, None, :], window_strides=(1,), padding=[(k - 1, 0)],
                                 dimension_numbers=("NWC", "WIO", "NWC"),
                                 feature_group_count=x.shape[-1])
    return y + b


def ssd_chunked(x, dt, a_head, bm, cm):
    b, L, H, P = x.shape
    G, N, E, Q = SSD_GROUPS, SSD_STATE, SSD_HEADS_PER_GROUP, SSD_CHUNK
    nc = L // Q
    X = (x.astype(jnp.float32) * dt[..., None]).reshape(b, nc, Q, G, E, P)
    a = (dt * a_head).reshape(b, nc, Q, G, E).transpose(0, 3, 4, 1, 2)
    Bc = bm.astype(jnp.float32).reshape(b, nc, Q, G, N)
    Cc = cm.astype(jnp.float32).reshape(b, nc, Q, G, N)
    a_cum = jnp.cumsum(a, axis=-1)
    seg = a_cum[..., :, None] - a_cum[..., None, :]
    causal = jnp.tril(jnp.ones((Q, Q), dtype=bool))
    Lmat = jnp.exp(jnp.where(causal, seg, -jnp.inf))
    CB = jnp.einsum('bclgn,bcsgn->bgcls', Cc, Bc)
    y_diag = jnp.einsum('bgcls,bgecls,bcsgep->bclgep', CB, Lmat, X)
    decay_states = jnp.exp(a_cum[..., -1:] - a_cum)
    states = jnp.einsum('bclgn,bgecl,bclgep->bcgepn', Bc, decay_states, X)
    chunk_decay = jnp.exp(a_cum[..., -1])

    def step(h, inp):
        s, d = inp
        return h * d[..., None, None] + s, h

    _, prev = lax.scan(step, jnp.zeros(states.shape[:1] + states.shape[2:], jnp.float32),
                       (jnp.moveaxis(states, 1, 0), jnp.moveaxis(chunk_decay, 3, 0)))
    prev = jnp.moveaxis(prev, 0, 1)
    y_off = jnp.einsum('bclgn,bcgepn,bgecl->bclgep', Cc, prev, jnp.exp(a_cum))
    return (y_diag + y_off).reshape(b, L, H, P)


def ssd_branch(z, xbc, dt_raw, conv_w, conv_b, dt_bias, a_log, d_skip, norm_g, w_branch):
    b, L, _ = z.shape
    xbc = jax.nn.silu(causal_dwconv(xbc, conv_w, conv_b))
    xs, bm, cm = jnp.split(xbc, [SSD_D_INNER, SSD_D_INNER + SSD_GROUPS * SSD_STATE], axis=-1)
    dt = jax.nn.softplus(dt_raw.astype(jnp.float32) + dt_bias.astype(jnp.float32))
    a_head = -jnp.exp(a_log.astype(jnp.float32))
    xh = xs.reshape(b, L, SSD_HEADS, SSD_HEAD_DIM)
    y = ssd_chunked(xh, dt, a_head,
                    bm.reshape(b, L, SSD_GROUPS, SSD_STATE), cm.reshape(b, L, SSD_GROUPS, SSD_STATE))
    y = (y + d_skip.astype(jnp.float32)[:, None] * xh.astype(jnp.float32)).astype(z.dtype)
    y = y.reshape(b, L, SSD_D_INNER) * jax.nn.silu(z)
    y = rmsnorm(y.reshape(b, L, SSD_GROUPS, SSD_D_INNER // SSD_GROUPS), 1.0).reshape(b, L, SSD_D_INNER) * norm_g
    return y @ w_branch


def diff_attn_branch(q, k, v, lq1, lk1, lq2, lk2, subln_g, w_branch, lam_init):
    b, L, _ = q.shape
    H, d = DA_HEADS, DA_HEAD_DIM
    q = q.reshape(b, L, H, 2, d)
    k = k.reshape(b, L, H, 2, d)
    v = v.reshape(b, L, H, DA_V_DIM)
    lam = (jnp.exp(jnp.sum(lq1.astype(jnp.float32) * lk1.astype(jnp.float32)))
           - jnp.exp(jnp.sum(lq2.astype(jnp.float32) * lk2.astype(jnp.float32))) + lam_init)
    slopes = alibi_slopes(H)
    scale = 1.0 / math.sqrt(d)
    nb = L // Q_BLOCK
    qb = q.reshape(b, nb, Q_BLOCK, H, 2, d).transpose(1, 0, 2, 3, 4, 5)
    key_pos = jnp.arange(L)

    def one_block(args):
        q_blk, i = args
        q_pos = i * Q_BLOCK + jnp.arange(Q_BLOCK)
        s = jnp.einsum('bqhmd,bkhmd->bhmqk', q_blk, k).astype(jnp.float32) * scale
        dist = (q_pos[:, None] - key_pos[None, :]).astype(jnp.float32)
        s = s - slopes[None, :, None, None, None] * dist
        s = jnp.where(dist >= 0, s, -jnp.inf)
        p = jax.nn.softmax(s, axis=-1)
        attn = p[:, :, 0] - lam * p[:, :, 1]
        return jnp.einsum('bhqk,bkhe->bqhe', attn.astype(v.dtype), v)

    out = lax.map(one_block, (qb, jnp.arange(nb)))
    out = out.transpose(1, 0, 2, 3, 4).reshape(b, L, H, DA_V_DIM)
    out = rmsnorm(out, subln_g) * (1.0 - lam_init)
    return out.reshape(b, L, DA_WIDTH) @ w_branch


def setup_inputs(seed: int = 0) -> dict:
    key = jax.random.key(seed)
    ks = iter(jax.random.split(key, 40))
    f32 = jnp.float32

    def nrm(shape, fan_in):
        return jax.random.normal(next(ks), (DEPTH,) + shape, f32) * fan_in ** -0.5

    def gain(n):
        return 1.0 + 0.05 * jax.random.normal(next(ks), (DEPTH, n), f32)

    x = jax.random.normal(next(ks), (BATCH, SEQ, D_MODEL), f32)
    dt0 = jnp.exp(jax.random.uniform(next(ks), (DEPTH, SSD_HEADS), f32)
                  * (math.log(SSD_DT_MAX) - math.log(SSD_DT_MIN)) + math.log(SSD_DT_MIN))
    dt0 = jnp.maximum(dt0, 1e-4)
    return {
        "x": x,
        "ffn1_pre_g": gain(D_MODEL),
        "ffn1_w_gate": nrm((D_MODEL, D_FF), D_MODEL),
        "ffn1_w_up": nrm((D_MODEL, D_FF), D_MODEL),
        "ffn1_w_down": nrm((D_FF, D_MODEL), D_FF),
        "ffn1_post_g": gain(D_MODEL),
        "mix_pre_g": gain(D_MODEL),
        "w_in": nrm((D_MODEL, D_IN_PROJ), D_MODEL),
        "gate_b": 0.02 * jax.random.normal(next(ks), (DEPTH, N_BRANCH * D_MODEL), f32),
        "ssd_conv_w": nrm((SSD_CONV, SSD_CONV_CH), SSD_CONV),
        "ssd_conv_b": 0.02 * jax.random.normal(next(ks), (DEPTH, SSD_CONV_CH), f32),
        "ssd_dt_bias": dt0 + jnp.log(-jnp.expm1(-dt0)),
        "ssd_A_log": jnp.log(jax.random.uniform(next(ks), (DEPTH, SSD_HEADS), f32, 1.0, 16.0)),
        "ssd_D": gain(SSD_HEADS),
        "ssd_norm_g": gain(SSD_D_INNER),
        "ssd_w_branch": nrm((SSD_D_INNER, D_MODEL), SSD_D_INNER),
        "da_lambda_q1": 0.1 * jax.random.normal(next(ks), (DEPTH, DA_HEAD_DIM), f32),
        "da_lambda_k1": 0.1 * jax.random.normal(next(ks), (DEPTH, DA_HEAD_DIM), f32),
        "da_lambda_q2": 0.1 * jax.random.normal(next(ks), (DEPTH, DA_HEAD_DIM), f32),
        "da_lambda_k2": 0.1 * jax.random.normal(next(ks), (DEPTH, DA_HEAD_DIM), f32),
        "da_subln_g": gain(DA_V_DIM),
        "da_w_branch": nrm((DA_WIDTH, D_MODEL), DA_WIDTH),
        "w_out": nrm((D_MODEL, D_MODEL), D_MODEL),
        "mix_post_g": gain(D_MODEL),
        "ffn2_pre_g": gain(D_MODEL),
        "ffn2_w_gate": nrm((D_MODEL, D_FF), D_MODEL),
        "ffn2_w_up": nrm((D_MODEL, D_FF), D_MODEL),
        "ffn2_w_down": nrm((D_FF, D_MODEL), D_FF),
        "ffn2_post_g": gain(D_MODEL),
    }


def reference(x, ffn1_pre_g, ffn1_w_gate, ffn1_w_up, ffn1_w_down, ffn1_post_g,
              mix_pre_g, w_in, gate_b,
              ssd_conv_w, ssd_conv_b, ssd_dt_bias, ssd_A_log, ssd_D, ssd_norm_g, ssd_w_branch,
              da_lambda_q1, da_lambda_k1, da_lambda_q2, da_lambda_k2, da_subln_g, da_w_branch,
              w_out, mix_post_g,
              ffn2_pre_g, ffn2_w_gate, ffn2_w_up, ffn2_w_down, ffn2_post_g):
    h = x
    b, L, _ = x.shape
    for l in range(DEPTH):
        lam_init = 0.8 - 0.6 * math.exp(-0.3 * l)
        h = h + 0.5 * rmsnorm(swiglu(rmsnorm(h, ffn1_pre_g[l]), ffn1_w_gate[l], ffn1_w_up[l], ffn1_w_down[l]),
                              ffn1_post_g[l])
        u = rmsnorm(h, mix_pre_g[l])
        z, xbc, dt_raw, q, k, v, gate_logits = jnp.split(u @ w_in[l], IN_SPLITS, axis=-1)
        y_ssd = ssd_branch(z, xbc, dt_raw, ssd_conv_w[l], ssd_conv_b[l], ssd_dt_bias[l], ssd_A_log[l],
                           ssd_D[l], ssd_norm_g[l], ssd_w_branch[l])
        y_att = diff_attn_branch(q, k, v, da_lambda_q1[l], da_lambda_k1[l], da_lambda_q2[l], da_lambda_k2[l],
                                 da_subln_g[l], da_w_branch[l], lam_init)
        gates = jax.nn.sigmoid(gate_logits + gate_b[l]).reshape(b, L, N_BRANCH, D_MODEL)
        merged = gates[:, :, 0] * y_ssd + gates[:, :, 1] * y_att
        h = h + rmsnorm(merged @ w_out[l], mix_post_g[l])
        h = h + 0.5 * rmsnorm(swiglu(rmsnorm(h, ffn2_pre_g[l]), ffn2_w_gate[l], ffn2_w_up[l], ffn2_w_down[l]),
                              ffn2_post_g[l])
    return h
```

```python
import numpy as np
import concourse.bass as bass
import concourse.mybir as mybir
from concourse.bass_utils import run_bass_kernel_spmd

F32 = mybir.dt.float32
BF16 = mybir.dt.bfloat16
AF = mybir.ActivationFunctionType
ALU = mybir.AluOpType
AX = mybir.AxisListType

D = 1024
L = 2048
DFF = 2816
NF = DFF // 128
EPS = 1e-6
N_DMA_SEMS = 24
NEG = -30000.0
DBG = {"ng": 4, "ntb": 4, "stage": 99, "skip_ffn1": 0, "sub": 99}


class Buf:
    __slots__ = ("name", "last_w", "readers", "excl")

    def __init__(self, name):
        self.name = name
        self.last_w = None
        self.readers = []
        self.excl = False


class Op:
    __slots__ = ("eng", "fn", "is_dma", "deps", "signal", "sem", "val", "clock", "idx", "inc", "own", "key", "clear")

    def __init__(self, eng, fn, is_dma):
        self.eng = eng
        self.fn = fn
        self.is_dma = is_dma
        self.deps = []
        self.signal = False
        self.sem = None
        self.val = 0
        self.clock = None
        self.inc = 1
        self.own = None
        self.key = None
        self.clear = False


class Sched:
    ENGINES = ("pe", "act", "dve", "pool", "sp")

    def __init__(self, nc):
        self.nc = nc
        self.ops = []
        self.dma_ops = []

    def add(self, eng, fn, reads=(), writes=(), is_dma=False, extra=(), own=None):
        op = Op(eng, fn, is_dma)
        op.idx = len(self.ops)
        op.own = own
        deps = {}

        def dep(p, raw):
            if p is None:
                return
            if (not p.is_dma) and (not is_dma) and p.eng == eng:
                if eng == "pe" or not raw:
                    return
            deps[p.idx] = p

        reads = list(reads)
        writes = list(writes)
        for b in list(reads):
            if b.excl:
                reads.remove(b)
                if b not in writes:
                    writes.append(b)
        for b in reads:
            dep(b.last_w, True)
        for b in writes:
            dep(b.last_w, False)
            for r in b.readers:
                dep(r, False)
        for p in extra:
            deps[p.idx] = p
        for b in reads:
            b.readers.append(op)
        for b in writes:
            b.last_w = op
            b.readers = []
        if is_dma and own is None:
            n = len(self.dma_ops)
            if n >= N_DMA_SEMS:
                p = self.dma_ops[n - N_DMA_SEMS]
                deps[p.idx] = p
            self.dma_ops.append(op)
            op.signal = True
        elif is_dma:
            op.signal = True
        op.deps = list(deps.values())
        for p in op.deps:
            p.signal = True
        self.ops.append(op)
        return op

    def emit(self):
        nc = self.nc
        esem = {e: nc.alloc_semaphore("s_" + e) for e in self.ENGINES}
        dsem = [nc.alloc_semaphore("s_dma%d" % i) for i in range(N_DMA_SEMS)]
        cnt = {e: 0 for e in self.ENGINES}
        ndma = 0
        own_sems = {}
        for op in self.ops:
            if op.is_dma and op.own is not None:
                if DBG.get("fresh"):
                    own_sems[id(op.own)] = [nc.alloc_semaphore("s_f%d" % op.idx), 0]
                if id(op.own) not in own_sems:
                    own_sems[id(op.own)] = [nc.alloc_semaphore("s_slot%d" % len(own_sems)), 0]
                ent = own_sems[id(op.own)]
                ent[1] += 1
                op.sem = ent[0]
                op.val = 16 * ent[1]
                op.inc = 16
                op.key = id(op.sem)
                op.clear = False
            elif op.is_dma:
                op.sem = dsem[ndma % N_DMA_SEMS]
                op.val = 16 * (ndma // N_DMA_SEMS + 1)
                op.inc = 16
                op.key = id(op.sem)
                ndma += 1
            elif op.signal:
                cnt[op.eng] += 1
                op.sem = esem[op.eng]
                op.val = cnt[op.eng]
                op.key = id(op.sem)
        known = {e: {} for e in self.ENGINES}
        waits = {}
        for op in self.ops:
            k = known[op.eng]
            w = []
            for p in sorted(op.deps, key=lambda p: -p.idx):
                sid = p.key
                if k.get(sid, (None, 0))[1] >= p.val:
                    continue
                w.append((p.sem, p.val))
                for s2, v2 in p.clock.items():
                    if k.get(s2, (None, 0))[1] < v2[1]:
                        k[s2] = v2
            waits[op.idx] = w
            if op.signal:
                c = dict(k)
                c[op.key] = (op.sem, op.val)
                op.clock = c
        per_eng = {e: [op for op in self.ops if op.eng == e] for e in self.ENGINES}
        self.stats = {e: len(per_eng[e]) for e in self.ENGINES}
        self.stats["waits"] = sum(len(w) for w in waits.values())

        def make(e):
            def body(eng):
                for op in per_eng[e]:
                    for (s, v) in waits[op.idx]:
                        eng.wait_ge(s, v)
                    if op.clear:
                        if op.key[1] > 1:
                            eng.wait_ge(op.sem, 16)
                        eng.sem_clear(op.sem)
                    ins = op.fn(eng)
                    if op.signal:
                        ins.then_inc(op.sem, op.inc)
            return body

        with nc.Block() as block:
            block.tensor(make("pe"))
            block.scalar(make("act"))
            block.vector(make("dve"))
            block.gpsimd(make("pool"))
            block.sync(make("sp"))


class T:
    __slots__ = ("ap", "buf")

    def __init__(self, ap, name):
        self.ap = ap
        self.buf = Buf(name)


def bcast(ap, shape, axis):
    return ap.unsqueeze(axis).broadcast_to(list(shape))


def build_program(debug=False, stop_after=None):
    nc = bass.Bass("TRN2", target_bir_lowering=False)
    S = Sched(nc)

    def din(name, shape):
        return nc.dram_tensor(name, list(shape), F32, kind="ExternalInput").ap()

    xT_d = din("xT", [D, L])
    w_d = {}
    for pre in ("ffn1", "ffn2"):
        w_d[pre + "_wg"] = din(pre + "_wg", [NF, 128, 8, 128])
        w_d[pre + "_wu"] = din(pre + "_wu", [NF, 128, 8, 128])
        w_d[pre + "_wd"] = din(pre + "_wd", [8, 128, NF, 128])
    wgrp_d = din("w_grp", [4, 128, 8, 1280])
    wdt_d = din("w_dt", [128, 8, 32])
    whead_d = din("w_head", [8, 128, 3, 8, 128])
    wtail_d = din("w_tail", [8, 128, 40, 128])
    wout_d = din("w_outb", [8, 128, 8, 128])
    gains_d = din("gains", [128, 6, 8])
    gateb_d = din("gate_b", [128, 16])
    convw_d = din("conv_w", [128, 24, 4])
    convb_d = din("conv_b", [128, 24])
    hv_d = din("headvec", [128, 3, 32])
    normg_d = din("ssd_norm_g", [128, 2048])
    subln_d = din("subln_g", [128, 1])
    lamv_d = din("lamv", [128, 4, 64])
    cst_d = din("consts", [128, 4, 128])
    ali2_d = din("alibi2", [128, 128])
    outT_d = nc.dram_tensor("outT", [D, L], F32, kind="ExternalOutput").ap()
    skind = "ExternalOutput" if debug else "Internal"
    ynT_d = nc.dram_tensor("ynT", [2048, L], BF16, kind=skind).ap()
    aoT_d = nc.dram_tensor("aoT", [1024, L], BF16, kind=skind).ap()
    ynT_t = T(ynT_d, "ynT_d")
    aoT_t = T(aoT_d, "aoT_d")
    dbg_d = {}
    if debug:
        for nm in ("h1T", "h2T"):
            dbg_d[nm] = nc.dram_tensor(nm, [D, L], F32, kind="ExternalOutput").ap()
    out_ops = []

    NW = 52800
    arena = nc.alloc_sbuf_tensor("arena", [128, NW], F32)
    top = [0]
    alloc_log = []

    def alloc(shape, dt, name):
        n = int(np.prod(shape))
        nb = n * (4 if dt == F32 else 2)
        nb = (nb + 31) // 32 * 32
        off = top[0]
        top[0] += nb
        assert top[0] <= NW * 4, ("SBUF arena overflow", name, top[0])
        inherit = []
        for (s2, e2, t2) in alloc_log:
            if s2 < off + nb and off < e2:
                inherit.extend(t2.buf.readers)
                if t2.buf.last_w is not None:
                    inherit.append(t2.buf.last_w)
        a = arena[:, off // 4: off // 4 + nb // 4]
        if dt == BF16:
            a = a.bitcast(BF16)
        a = a[:, 0:n]
        if len(shape) == 2:
            a = a.rearrange("p (a b) -> p a b", a=shape[0])
        elif len(shape) == 3:
            a = a.rearrange("p (a b c) -> p a b c", a=shape[0], b=shape[1])
        t = T(a, name)
        t.buf.readers = inherit
        alloc_log.append((off, off + nb, t))
        return t

    def mark():
        return top[0]

    def release(m):
        top[0] = m

    PS = []
    for i in range(8):
        p = nc.alloc_psum_tensor("ps%d" % i, [128, 512], F32)
        PS.append(T(p[:], "ps%d" % i))
        PS[-1].buf.excl = True

    class Rot:
        def __init__(self, items):
            self.items = items
            self.i = 0

        def next(self):
            it = self.items[self.i % len(self.items)]
            self.i += 1
            return it

    def dma(q, out_t, out_ap, in_t, in_ap):
        return S.add(q, lambda e: e.dma_start(out=out_ap, in_=in_ap),
                     reads=[in_t.buf] if in_t is not None else [],
                     writes=[out_t.buf], is_dma=True, own=(out_t if q == "pool" else None))

    def mm(out_t, out_ap, l_ts, lhsT, r_ts, rhs, start, stop):
        S.add("pe", lambda e: e.matmul(out_ap, lhsT=lhsT, rhs=rhs, start=start, stop=stop),
              reads=[t.buf for t in l_ts] + [t.buf for t in r_ts], writes=[out_t.buf])

    def tr(out_t, out_ap, in_t, in_ap, ident_t, ident_ap):
        S.add("pe", lambda e: e.transpose(out_ap, in_ap, ident_ap),
              reads=[in_t.buf, ident_t.buf], writes=[out_t.buf])

    def act(out_t, out_ap, in_ts, in_ap, func, bias=None, scale=None, accum=None, extra_w=()):
        def f(e):
            kw = {}
            if bias is not None:
                kw["bias"] = bias
            if scale is not None:
                kw["scale"] = scale
            if accum is not None:
                kw["accum_out"] = accum
            return e.activation(out=out_ap, in_=in_ap, func=func, **kw)
        S.add("act", f, reads=[t.buf for t in in_ts], writes=[out_t.buf] + [t.buf for t in extra_w])

    def tt(out_t, out_ap, in_ts, in0, in1, op, eng="dve"):
        S.add(eng, lambda e: e.tensor_tensor(out=out_ap, in0=in0, in1=in1, op=op),
              reads=[t.buf for t in in_ts], writes=[out_t.buf])

    def stt(out_t, out_ap, in_ts, in0, scalar, in1, op0, op1, eng="dve"):
        S.add(eng, lambda e: e.scalar_tensor_tensor(out=out_ap, in0=in0, scalar=scalar, in1=in1, op0=op0, op1=op1),
              reads=[t.buf for t in in_ts], writes=[out_t.buf])

    def ts(out_t, out_ap, in_ts, in0, s1, s2, op0, op1=None, eng="dve"):
        def f(e):
            if op1 is None:
                return e.tensor_scalar(out=out_ap, in0=in0, scalar1=s1, scalar2=None, op0=op0)
            return e.tensor_scalar(out=out_ap, in0=in0, scalar1=s1, scalar2=s2, op0=op0, op1=op1)
        S.add(eng, f, reads=[t.buf for t in in_ts], writes=[out_t.buf])

    def cp(out_t, out_ap, in_ts, in_ap, eng="dve"):
        if eng == "act":
            act(out_t, out_ap, in_ts, in_ap, AF.Copy)
        else:
            S.add(eng, lambda e: e.tensor_copy(out=out_ap, in_=in_ap),
                  reads=[t.buf for t in in_ts], writes=[out_t.buf])

    def recip(out_t, out_ap, in_ts, in_ap):
        S.add("dve", lambda e: e.reciprocal(out=out_ap, in_=in_ap),
              reads=[t.buf for t in in_ts], writes=[out_t.buf])

    def memset(out_t, out_ap, val):
        S.add("dve", lambda e: e.memset(out_ap, val), writes=[out_t.buf])

    H = alloc([8, L], F32, "H")
    gains = alloc([6, 8], F32, "gains")
    gainsh = alloc([6, 8], F32, "gainsh")
    gateb = alloc([16], F32, "gateb")
    convw = alloc([24, 4], F32, "convw")
    convb = alloc([24], F32, "convb")
    hv = alloc([3, 32], F32, "hv")
    Aneg = alloc([32], F32, "Aneg")
    subln = alloc([1], F32, "subln")
    sg08 = alloc([1], F32, "sg08")
    lamv = alloc([4, 64], F32, "lamv")
    lamt = alloc([2, 64], F32, "lamt")
    lams = alloc([2], F32, "lams")
    lame = alloc([2], F32, "lame")
    nlam = alloc([1], F32, "nlam")
    cst = alloc([4, 128], F32, "cst")
    ali2 = alloc([128], F32, "ali2")
    ident_f = T(cst.ap[:, 0, :], "ident_f"); ident_f.buf = cst.buf
    tri_f = T(cst.ap[:, 1, :], "tri_f"); tri_f.buf = cst.buf
    ident_b = alloc([128], BF16, "ident_b")
    negm_b = alloc([128], BF16, "negm_b")
    ones_b = alloc([128], BF16, "ones_b")
    tri_b = alloc([128], BF16, "tri_b")
    epsb = alloc([1], F32, "epsb")
    onef = alloc([1], F32, "onef")

    dma("sp", gains, gains.ap, None, gains_d)
    dma("sp", gateb, gateb.ap, None, gateb_d)
    dma("sp", convw, convw.ap, None, convw_d)
    dma("sp", convb, convb.ap, None, convb_d)
    dma("sp", hv, hv.ap, None, hv_d)
    dma("sp", subln, subln.ap, None, subln_d)
    dma("sp", lamv, lamv.ap, None, lamv_d)
    dma("sp", cst, cst.ap, None, cst_d)
    dma("sp", ali2, ali2.ap, None, ali2_d)
    xT_v = xT_d.rearrange("(c p) t -> p c t", p=128)
    H_ld = []
    for tb in range(4):
        t_ = T(H.ap, "Hld%d" % tb)
        dma("sp", t_, H.ap[:, :, tb * 512:(tb + 1) * 512], None, xT_v[:, :, tb * 512:(tb + 1) * 512])
        H_ld.append(t_)
    ts(gainsh, gainsh.ap, [gains], gains.ap, 0.5, None, ALU.mult)
    cp(ident_b, ident_b.ap, [cst], cst.ap[:, 0, :])
    cp(negm_b, negm_b.ap, [cst], cst.ap[:, 2, :])
    cp(tri_b, tri_b.ap, [cst], cst.ap[:, 1, :])
    memset(ones_b, ones_b.ap, 1.0)
    memset(epsb, epsb.ap, EPS)
    memset(onef, onef.ap, 1.0)
    act(Aneg, Aneg.ap, [hv], hv.ap[:, 1, :], AF.Exp)
    ts(Aneg, Aneg.ap, [Aneg], Aneg.ap, -1.0, None, ALU.mult)
    ts(sg08, sg08.ap, [subln], subln.ap, 0.8, None, ALU.mult)
    tt(lamt, lamt.ap[:, 0, :], [lamv], lamv.ap[:, 0, :], lamv.ap[:, 1, :], ALU.mult)
    tt(lamt, lamt.ap[:, 1, :], [lamv], lamv.ap[:, 2, :], lamv.ap[:, 3, :], ALU.mult)
    S.add("dve", lambda e: e.tensor_reduce(out=lams.ap, in_=lamt.ap, axis=AX.X, op=ALU.add),
          reads=[lamt.buf], writes=[lams.buf])
    act(lame, lame.ap, [lams], lams.ap, AF.Exp)
    tt(nlam, nlam.ap, [lame], lame.ap[:, 1:2], lame.ap[:, 0:1], ALU.subtract)
    ts(nlam, nlam.ap, [nlam], nlam.ap, -0.2, None, ALU.add)

    alia = T(cst.ap[:, 3, :], "alia"); alia.buf = cst.buf

    def rms_stats(src_ts, src_aps, nfeat, ps_t, sq_rot, rstd_t, rstd_ap):
        n = len(src_aps)
        W = src_aps[0].shape[-1]
        for c in range(n):
            sq = sq_rot.next()
            act(sq, sq.ap[:, 0:W], src_ts, src_aps[c], AF.Square)
            mm(ps_t, ps_t.ap[:, 0:W], [ones_b], ones_b.ap, [sq], sq.ap[:, 0:W], c == 0, c == n - 1)
        act(rstd_t, rstd_ap, [ps_t, epsb], ps_t.ap[:, 0:W], AF.Ln, bias=epsb.ap, scale=1.0 / nfeat)
        act(rstd_t, rstd_ap, [rstd_t], rstd_ap, AF.Exp, scale=-0.5)

    def dump_H(name):
        if debug:
            for c in range(8):
                t_ = T(dbg_d[name], name + str(c))
                op = dma("sp", t_, dbg_d[name][c * 128:(c + 1) * 128, :], H, H.ap[:, c, :])
                out_ops.append(op)

    def ffn(pre, gi_pre, gi_post, ld=None, out_hook=None):
        m0 = mark()
        XN = alloc([8, 1024], BF16, "XN")
        ACTB = alloc([NF, 1024], BF16, "ACTB")
        Y = alloc([8, 1024], F32, "Y")
        wslots = Rot([alloc([NF, 128], BF16, "wslot%d" % i) for i in range(4)])
        rstd_rot = Rot([alloc([512], F32, "rstd%d" % i) for i in range(2)])
        sq_rot = Rot([alloc([512], BF16, "sq%d" % i) for i in range(2)])
        sil_rot = Rot([alloc([512], F32, "sil%d" % i) for i in range(2)])
        tmp_rot = Rot([alloc([512], F32, "tmp%d" % i) for i in range(3)])
        up_rot = Rot([(PS[0], PS[1]), (PS[2], PS[3])])
        dn_rot = Rot([PS[4], PS[5]])
        st_rot = Rot([PS[6], PS[7]])
        wg_d, wu_d, wd_d = w_d[pre + "_wg"], w_d[pre + "_wu"], w_d[pre + "_wd"]
        def pre_norm(half):
            for tb in range(2):
                t0 = half * 1024 + tb * 512
                rstd = rstd_rot.next()
                hrd = [H] + ([ld[half * 2 + tb]] if ld is not None else [])
                rms_stats(hrd, [H.ap[:, c, t0:t0 + 512] for c in range(8)], D, st_rot.next(), sq_rot, rstd, rstd.ap)
                for c in range(8):
                    stt(XN, XN.ap[:, c, tb * 512:(tb + 1) * 512], hrd + [gains, rstd], H.ap[:, c, t0:t0 + 512],
                        gains.ap[:, gi_pre, c:c + 1], rstd.ap, ALU.mult, ALU.mult)

        def up(hook):
            for f in range(NF):
                wg = wslots.next()
                dma("pool", wg, wg.ap[:, 0:8, :], None, wg_d[f])
                wu = wslots.next()
                dma("pool", wu, wu.ap[:, 0:8, :], None, wu_d[f])
                for tb in range(2):
                    pg, pu = up_rot.next()
                    cs = slice(tb * 512, (tb + 1) * 512)
                    for k in range(8):
                        mm(pg, pg.ap, [wg], wg.ap[:, k, :], [XN], XN.ap[:, k, cs], k == 0, k == 7)
                    for k in range(8):
                        mm(pu, pu.ap, [wu], wu.ap[:, k, :], [XN], XN.ap[:, k, cs], k == 0, k == 7)
                    sil = sil_rot.next()
                    act(sil, sil.ap, [pg], pg.ap, AF.Silu)
                    tt(ACTB, ACTB.ap[:, f, cs], [sil, pu], sil.ap, pu.ap, ALU.mult)
                if hook:
                    hook.pop(0)()

        def down():
            stb = [st_rot.next(), st_rot.next()]
            for d in range(8):
                wd = wslots.next()
                dma("pool", wd, wd.ap, None, wd_d[d])
                for tb in range(2):
                    ps = dn_rot.next()
                    cs = slice(tb * 512, (tb + 1) * 512)
                    for f in range(NF):
                        mm(ps, ps.ap, [wd], wd.ap[:, f, :], [ACTB], ACTB.ap[:, f, cs], f == 0, f == NF - 1)
                    cp(Y, Y.ap[:, d, cs], [ps], ps.ap, eng="act")
                    sq = sq_rot.next()
                    act(sq, sq.ap, [ps], ps.ap, AF.Square)
                    mm(stb[tb], stb[tb].ap, [ones_b], ones_b.ap, [sq], sq.ap, d == 0, d == 7)
            return stb

        def post_units(half, stb):
            units = []
            for tb in range(2):
                t0 = half * 1024 + tb * 512
                cs = slice(tb * 512, (tb + 1) * 512)
                rstd = rstd_rot.next()
                act(rstd, rstd.ap, [stb[tb], epsb], stb[tb].ap, AF.Ln, bias=epsb.ap, scale=1.0 / D)
                act(rstd, rstd.ap, [rstd], rstd.ap, AF.Exp, scale=-0.5)
                for c in range(8):
                    def unit(c=c, t0=t0, cs=cs, rstd=rstd):
                        tmp = tmp_rot.next()
                        stt(tmp, tmp.ap, [Y, gainsh, rstd], Y.ap[:, c, cs], gainsh.ap[:, gi_post, c:c + 1], rstd.ap,
                            ALU.mult, ALU.mult)
                        tt(H, H.ap[:, c, t0:t0 + 512], [H, tmp], H.ap[:, c, t0:t0 + 512], tmp.ap, ALU.add,
                           eng=("pool" if c % 3 == 2 else "dve"))
                    units.append(unit)
            return units

        pre_norm(0)
        up(None)
        pre_norm(1)
        stb0 = down()
        units = post_units(0, stb0)
        up(units)
        for u_ in units[:]:
            units.pop(0)()
        if out_hook is not None:
            out_hook()
        stb1 = down()
        for u_ in post_units(1, stb1):
            u_()
        release(m0)

    if not DBG["skip_ffn1"]:
        ffn("ffn1", 0, 1, ld=H_ld)
    dump_H("h1T")

    def mixer():
        mU = mark()
        U = alloc([8, L], BF16, "U")
        m1 = mark()
        rstd_rot = Rot([alloc([512], F32, "rstd%d" % i) for i in range(2)])
        sq_rot = Rot([alloc([512], BF16, "sq%d" % i) for i in range(2)])
        st_rot = Rot([PS[6], PS[7]])
        for tb in range(4):
            t0 = tb * 512
            rstd = rstd_rot.next()
            rms_stats([H], [H.ap[:, c, t0:t0 + 512] for c in range(8)], D, st_rot.next(), sq_rot, rstd, rstd.ap)
            for c in range(8):
                stt(U, U.ap[:, c, t0:t0 + 512], [H, gains, rstd], H.ap[:, c, t0:t0 + 512],
                    gains.ap[:, 2, c:c + 1], rstd.ap, ALU.mult, ALU.mult)
        release(m1)

        mS = mark()
        wdt = alloc([8, 32], BF16, "wdt")
        dt_all = alloc([16, 32], F32, "dt_all")
        a_all = alloc([16, 32], F32, "a_all")
        a_hi = alloc([16, 32], BF16, "a_hi")
        a_lo = alloc([16, 32], BF16, "a_lo")
        na_hi = alloc([16, 32], BF16, "na_hi")
        na_lo = alloc([16, 32], BF16, "na_lo")
        mtmp = mark()
        tA = alloc([16, 32], F32, "tA")
        tB = alloc([16, 32], F32, "tB")
        dma("pool", wdt, wdt.ap, None, wdt_d)
        psd = PS[0]
        for tc in range(16):
            for k in range(8):
                mm(psd, psd.ap[:, tc * 32:(tc + 1) * 32], [U], U.ap[:, k, tc * 128:(tc + 1) * 128],
                   [wdt], wdt.ap[:, k, :], k == 0, k == 7)
        psd3 = psd.ap.rearrange("p (a b) -> p a b", a=16)
        tt(tA, tA.ap, [psd, hv], psd3, bcast(hv.ap[:, 0, :], [128, 16, 32], 1), ALU.add)
        stt(tB, tB.ap, [tA], tA.ap, -1.0, tA.ap, ALU.mult, ALU.max)
        act(tB, tB.ap, [tB], tB.ap, AF.Exp, scale=-1.0)
        act(tB, tB.ap, [tB, onef], tB.ap, AF.Ln, bias=onef.ap, scale=1.0)
        stt(dt_all, dt_all.ap, [tA, tB], tA.ap, 0.0, tB.ap, ALU.max, ALU.add)
        tt(a_all, a_all.ap, [dt_all, Aneg], dt_all.ap, bcast(Aneg.ap, [128, 16, 32], 1), ALU.mult)
        cp(a_hi, a_hi.ap, [a_all], a_all.ap)
        tt(tA, tA.ap, [a_all, a_hi], a_all.ap, a_hi.ap, ALU.subtract)
        cp(a_lo, a_lo.ap, [tA], tA.ap)
        ts(na_hi, na_hi.ap, [a_hi], a_hi.ap, -1.0, None, ALU.mult)
        ts(na_lo, na_lo.ap, [a_lo], a_lo.ap, -1.0, None, ALU.mult)
        release(mtmp)

        WG = alloc([8, 1280], BF16, "WG")
        normg = alloc([512], F32, "normg")
        wdiag = alloc([6, 4, 128], BF16, "wdiag")
        prevT = alloc([512], F32, "prevT")
        prevTb = alloc([512], BF16, "prevTb")
        Et = alloc([8, 128], F32, "E")

        class Bufs:
            pass
        blk = []
        for i in range(2):
            bb = Bufs()
            bb.xpre = [alloc([516], BF16, "xpre%d_%d" % (i, j)) for j in range(6)]
            bb.xsT = alloc([4, 512], BF16, "xsT%d" % i)
            bb.BT = alloc([512], BF16, "BT%d" % i)
            bb.CT = alloc([512], BF16, "CT%d" % i)
            bb.zs = alloc([4, 512], BF16, "zs%d" % i)
            blk.append(bb)
        cbs = []
        for i in range(2):
            cb = Bufs()
            cb.xs_tok = alloc([512], BF16, "xs_tok%d" % i)
            cb.Xdt = alloc([512], BF16, "Xdt%d" % i)
            cb.Xd = alloc([512], BF16, "Xd%d" % i)
            cb.Btok = alloc([128], BF16, "Btok%d" % i)
            cb.small = alloc([8, 8], F32, "small%d" % i)
            cb.MT = alloc([8, 128], BF16, "MT%d" % i)
            cb.yoff = alloc([512], F32, "yoff%d" % i)
            cb.dx = alloc([512], F32, "dx%d" % i)
            cb.y = alloc([512], F32, "y%d" % i)
            cb.yn = alloc([512], BF16, "yn%d" % i)
            cb.ynT = alloc([4, 128], BF16, "ynT%d" % i)
            cbs.append(cb)
        cb_rot = Rot(cbs)
        ip_rot = Rot([PS[0], PS[1]])
        psX, psM, psR0, psR1, psY, psYo = PS[2], PS[3], PS[4], PS[5], PS[6], PS[7]
        psT = psY
        ynT_v = ynT_d.rearrange("(c p) t -> p c t", p=128)
        v3 = lambda ap: ap.rearrange("p (a b) -> p a b", a=8)

        def group_fns(g):
            chs = [4 * g + j if j < 4 else (16 + g if j == 4 else 20 + g) for j in range(6)]

            def group_setup():
                dma("pool", WG, WG.ap, None, wgrp_d[g])
                for j in range(6):
                    for k in range(4):
                        ts(wdiag, wdiag.ap[:, j, k, :], [ident_b, convw], ident_b.ap, convw.ap[:, chs[j], k:k + 1], None,
                           ALU.mult)

            def group_begin():
                dma("sp", normg, normg.ap, None, normg_d[:, g * 512:(g + 1) * 512])
                memset(prevT, prevT.ap, 0.0)
                memset(prevTb, prevTb.ap, 0.0)

            def inproj(tb):
                bb = blk[tb % 2]
                pb = blk[(tb - 1) % 2]
                t0 = tb * 512
                st = {}

                def mm_a(j):
                    ps = ip_rot.next()
                    for k in range(8):
                        mm(ps, ps.ap, [WG], WG.ap[:, k, 512 + j * 128:512 + (j + 1) * 128],
                           [U], U.ap[:, k, t0:t0 + 512], k == 0, k == 7)
                    st[("a", j)] = ps

                def evac_a(j):
                    ps = st.pop(("a", j))
                    xp = bb.xpre[j]
                    cp(xp, xp.ap[:, 3:515], [ps], ps.ap, eng="act")
                    if tb == 0:
                        memset(xp, xp.ap[:, 0:3], 0.0)
                    else:
                        cp(xp, xp.ap[:, 0:3], [pb.xpre[j]], pb.xpre[j].ap[:, 512:515])

                def conv(j):
                    xp = bb.xpre[j]
                    ps2 = ip_rot.next()
                    for k in range(4):
                        mm(ps2, ps2.ap, [wdiag], wdiag.ap[:, j, k, :], [xp], xp.ap[:, k:k + 512], k == 0, k == 3)
                    st[("c", j)] = ps2

                def silu_c(j):
                    ps2 = st.pop(("c", j))
                    if j < 4:
                        dst, dap = bb.xsT, bb.xsT.ap[:, j, :]
                    elif j == 4:
                        dst, dap = bb.BT, bb.BT.ap
                    else:
                        dst, dap = bb.CT, bb.CT.ap
                    act(dst, dap, [ps2, convb], ps2.ap, AF.Silu, bias=convb.ap[:, chs[j]:chs[j] + 1], scale=1.0)

                def mm_z(cc):
                    ps = ip_rot.next()
                    l0 = t0 + cc * 128
                    for k in range(8):
                        mm(ps, ps.ap, [U], U.ap[:, k, l0:l0 + 128], [WG], WG.ap[:, k, 0:512], k == 0, k == 7)
                    st[("z", cc)] = ps

                def silu_z(cc):
                    ps = st.pop(("z", cc))
                    act(bb.zs, bb.zs.ap[:, cc, :], [ps], ps.ap, AF.Silu)

                def part1():
                    mm_a(0)
                    yield
                    for j in range(6):
                        if j + 1 < 6:
                            mm_a(j + 1)
                        evac_a(j)
                        yield

                def part2():
                    seq = [("c", j) for j in range(6)] + [("z", cc) for cc in range(4)]
                    def issue(u):
                        (conv if u[0] == "c" else mm_z)(u[1])
                    def finish(u):
                        (silu_c if u[0] == "c" else silu_z)(u[1])
                    issue(seq[0])
                    for i, u in enumerate(seq):
                        if i + 1 < len(seq):
                            issue(seq[i + 1])
                        finish(u)
                return part1(), part2

            def stage_a(tb, cc, cb):
                bb = blk[tb % 2]
                c = tb * 4 + cc
                cs = slice(cc * 128, (cc + 1) * 128)
                sm = cb.small
                acum, cdarg, cd, ea = (sm.ap[:, i, :] for i in range(4))
                dt_g = dt_all.ap[:, c, 8 * g:8 * g + 8]
                hs = slice(8 * g, 8 * g + 8)
                psXb = psX.ap.bitcast(BF16)
                for j in range(4):
                    tr(psX, psXb[:, j * 128:(j + 1) * 128], bb.xsT, bb.xsT.ap[:, j, cs], ident_b, ident_b.ap)
                psMb = psM.ap.bitcast(BF16)
                tr(psM, psMb[:, 0:128], bb.BT, bb.BT.ap[:, cs], ident_b, ident_b.ap)
                mm(psM, psM.ap[:, 128:136], [tri_b], tri_b.ap, [a_hi], a_hi.ap[:, c, hs], True, False)
                mm(psM, psM.ap[:, 128:136], [tri_b], tri_b.ap, [a_lo], a_lo.ap[:, c, hs], False, True)
                mm(psM, psM.ap[:, 256:384], [bb.BT], bb.BT.ap[:, cs], [bb.CT], bb.CT.ap[:, cs], True, True)
                yield
                for half, pr in ((0, psR0), (1, psR1)):
                    h4 = slice(8 * g + 4 * half, 8 * g + 4 * half + 4)
                    pr3 = pr.ap.rearrange("p (a b) -> p a b", a=4)
                    mm(pr, pr3, [ident_b], ident_b.ap, [negm_b], bcast(negm_b.ap, [128, 4, 128], 1), True, False)
                    mm(pr, pr3, [tri_b], tri_b.ap, [na_hi], bcast(na_hi.ap[:, c, h4], [128, 4, 128], 2), False, False)
                    mm(pr, pr3, [tri_b], tri_b.ap, [na_lo], bcast(na_lo.ap[:, c, h4], [128, 4, 128], 2), False, False)
                    for q in range(4):
                        hh = 8 * g + 4 * half + q
                        o = pr.ap[:, q * 128:(q + 1) * 128]
                        mm(pr, o, [a_hi], a_hi.ap[:, c, hh:hh + 1].broadcast_to([128, 128]), [tri_b], tri_b.ap, False, False)
                        mm(pr, o, [a_lo], a_lo.ap[:, c, hh:hh + 1].broadcast_to([128, 128]), [tri_b], tri_b.ap, False, q == 3)
                    yield
                cp(cb.xs_tok, cb.xs_tok.ap, [psX], psXb[:, 0:512], eng="act")
                tt(cb.Xdt, v3(cb.Xdt.ap), [psX, dt_all], v3(psXb[:, 0:512]), bcast(dt_g, [128, 8, 64], 2), ALU.mult)
                cp(sm, acum, [psM], psM.ap[:, 128:136])
                cp(cb.Btok, cb.Btok.ap, [psM], psMb[:, 0:128], eng="act")
                yield
                Ev = Et.ap.rearrange("p a b -> p (a b)")
                act(Et, Ev[:, 0:512], [psR0], psR0.ap, AF.Exp)
                yield
                act(Et, Ev[:, 512:1024], [psR1], psR1.ap, AF.Exp)
                yield
                tt(sm, cdarg[:, 0:4], [psR0, sm], psR0.ap[:, 127::128], acum[:, 0:4], ALU.add)
                tt(sm, cdarg[:, 4:8], [psR1, sm], psR1.ap[:, 127::128], acum[:, 4:8], ALU.add)
                yield
                act(sm, cd, [sm], cdarg, AF.Exp)
                act(sm, ea, [sm], acum, AF.Exp)
                yield

            def stage_a2(tb, cc, cb):
                sm = cb.small
                tt(cb.MT, cb.MT.ap, [Et, psM], Et.ap, bcast(psM.ap[:, 256:384], [128, 8, 128], 1), ALU.mult)
                tt(cb.Xd, v3(cb.Xd.ap), [cb.Xdt, Et], v3(cb.Xdt.ap), bcast(Et.ap[:, :, 127], [128, 8, 64], 2),
                   ALU.mult, eng="pool")
                tt(cb.dx, v3(cb.dx.ap), [cb.xs_tok, hv], v3(cb.xs_tok.ap),
                   bcast(hv.ap[:, 2, 8 * g:8 * g + 8], [128, 8, 64], 2), ALU.mult, eng="pool")

            def stage_b(tb, cc, cb):
                bb = blk[tb % 2]
                cs = slice(cc * 128, (cc + 1) * 128)
                l0 = tb * 512 + cc * 128
                sm = cb.small
                acum, cdarg, cd, ea = (sm.ap[:, i, :] for i in range(4))
                ss, rs = sm.ap[:, 6, 0:1], sm.ap[:, 7, 0:1]
                mm(psYo, psYo.ap, [bb.CT], bb.CT.ap[:, cs], [prevTb], prevTb.ap, True, True)
                Xdt3 = v3(cb.Xdt.ap)
                for hh in range(8):
                    mm(psY, psY.ap[:, hh * 64:(hh + 1) * 64], [cb.MT], cb.MT.ap[:, hh, :], [cb.Xdt], Xdt3[:, hh, :],
                       True, True)
                yield
                tt(cb.yoff, v3(cb.yoff.ap), [psYo, sm], v3(psYo.ap), bcast(ea, [128, 8, 64], 2), ALU.mult)
                yield
                mm(psYo, psYo.ap, [cb.Btok], cb.Btok.ap, [cb.Xd], cb.Xd.ap, True, True)
                tt(prevT, v3(prevT.ap), [prevT, sm], v3(prevT.ap), bcast(cd, [128, 8, 64], 2), ALU.mult)
                tt(cb.y, cb.y.ap, [psY, cb.yoff], psY.ap, cb.yoff.ap, ALU.add)
                yield
                tt(prevT, prevT.ap, [prevT, psYo], prevT.ap, psYo.ap, ALU.add)
                tt(cb.y, cb.y.ap, [cb.y, cb.dx], cb.y.ap, cb.dx.ap, ALU.add)
                yield
                cp(prevTb, prevTb.ap, [prevT], prevT.ap, eng="act")
                tt(cb.y, cb.y.ap, [cb.y, bb.zs], cb.y.ap, bb.zs.ap[:, cc, :], ALU.mult)
                yield
                act(cb.yn, cb.yn.ap, [cb.y], cb.y.ap, AF.Square, accum=ss, extra_w=[sm])
                act(sm, rs, [sm, epsb], ss, AF.Ln, bias=epsb.ap, scale=1.0 / 512)
                act(sm, rs, [sm], rs, AF.Exp, scale=-0.5)
                yield
                stt(cb.yn, cb.yn.ap, [cb.y, sm, normg], cb.y.ap, rs, normg.ap, ALU.mult, ALU.mult)
                yield
                psTb = psT.ap.bitcast(BF16)
                for j in range(4):
                    tr(psT, psTb[:, j * 128:(j + 1) * 128], cb.yn, cb.yn.ap[:, j * 128:(j + 1) * 128], ident_b, ident_b.ap)
                yield
                cp(cb.ynT, cb.ynT.ap, [psT], psTb[:, 0:512].rearrange("p (a b) -> p a b", a=4), eng="act")
                dma("sp", ynT_t, ynT_v[:, 4 * g:4 * g + 4, l0:l0 + 128], cb.ynT, cb.ynT.ap)
                yield

            return group_setup, group_begin, inproj, stage_a, stage_a2, stage_b

        gf = [group_fns(g) for g in range(4)]
        gf[0][0]()
        def zipn(*gens):
            done = object()
            alive = [True] * len(gens)
            while any(alive):
                for i, g_ in enumerate(gens):
                    if alive[i]:
                        alive[i] = next(g_, done) is not done

        def run(gen):
            for _ in gen:
                pass

        def dec(k):
            return k // 16, (k // 4) % 4, k % 4

        def stA(k):
            g_, tb_, cc_ = dec(k)
            return gf[g_][3](tb_, cc_, cbs[k % 2])

        def stA2(k):
            g_, tb_, cc_ = dec(k)
            return gf[g_][4](tb_, cc_, cbs[k % 2])

        def stB(k):
            g_, tb_, cc_ = dec(k)
            return gf[g_][5](tb_, cc_, cbs[k % 2])

        gf[0][0]()
        _p1, _p2 = gf[0][2](0)
        run(_p1)
        _p2()
        gf[0][1]()
        run(stA(0))
        stA2(0)
        for k in range(1, 64):
            g, tb, cc = dec(k)
            bi = k // 4
            if tb == 3 and cc == 0 and g < 3:
                gf[g + 1][0]()
            gens = [stA(k), stB(k - 1)]
            p2 = None
            if cc == 2 and bi + 1 < 16:
                g2, tb2 = divmod(bi + 1, 4)
                p1, p2 = gf[g2][2](tb2)
                gens.append(p1)
            zipn(*gens)
            if k % 16 == 0:
                gf[g][1]()
            stA2(k)
            if p2 is not None:
                p2()
        run(stB(63))
        release(mS)
        if stop_after == "ssd":
            release(mU)
            return

        mA = mark()
        wh_rot = Rot([alloc([3, 8, 128], BF16, "wh%d" % i) for i in range(2)])
        qk_rot = Rot([(alloc([L], BF16, "qT%d" % i), alloc([L], BF16, "kz0_%d" % i), alloc([L], BF16, "kz1_%d" % i))
                      for i in range(2)])
        for (_q, _k0, _k1) in qk_rot.items:
            memset(_k0, _k0.ap[64:128, :], 0.0)
            memset(_k1, _k1.ap[0:64, :], 0.0)
        vT = alloc([L], BF16, "vT")
        vtok_rot = Rot([alloc([16, 128], BF16, "vtok%d" % i) for i in range(2)])
        ET_rot = Rot([alloc([512], BF16, "ET%d" % i) for i in range(4)])
        nrm = [alloc([512], F32, "nrm%d" % i) for i in range(4)]
        sqa = alloc([512], BF16, "sqa")
        ao_rot = Rot([alloc([512], BF16, "ao%d" % i) for i in range(2)])
        g_rot = Rot([PS[0], PS[1], PS[2], PS[3]])
        psO = [PS[4], PS[5]]
        psZ = [PS[6], PS[7]]
        ev = Rot(["act", "dve"])
        for h in range(8):
            wh = wh_rot.next()
            dma("pool", wh, wh.ap, None, whead_d[h])
            qT, kz0, kz1 = qk_rot.next()
            kz = (kz0, kz1)
            for i, dst in enumerate((qT, None, vT)):
                for tb in range(4):
                    ps = g_rot.next()
                    cs = slice(tb * 512, (tb + 1) * 512)
                    for k in range(8):
                        mm(ps, ps.ap, [wh], wh.ap[:, i, k, :], [U], U.ap[:, k, cs], k == 0, k == 7)
                    if dst is None:
                        cp(kz0, kz0.ap[0:64, cs], [ps], ps.ap[0:64, :], eng="act")
                        cp(kz1, kz1.ap[64:128, cs], [ps], ps.ap[64:128, :], eng="dve")
                    else:
                        cp(dst, dst.ap[:, cs], [ps], ps.ap, eng=ev.next())
            vtok = vtok_rot.next()
            for t4 in range(4):
                ps = g_rot.next()
                psb = ps.ap.bitcast(BF16)
                for j in range(4):
                    tcn = t4 * 4 + j
                    tr(ps, psb[:, j * 128:(j + 1) * 128], vT, vT.ap[:, tcn * 128:(tcn + 1) * 128], ident_b, ident_b.ap)
                cp(vtok, vtok.ap[:, t4 * 4:(t4 + 1) * 4, :], [ps], psb[:, 0:512].rearrange("p (a b) -> p a b", a=4),
                   eng=ev.next())
            wide = h >= 2
            tasks = [(I, m, j) for I in range(4) for m in range(2) for j in range(4 * (I + 1))]
            tstate = {}

            def st_score(t):
                I, m, j = t
                q0 = 512 * I
                rows = slice(m * 64, (m + 1) * 64)
                diag = j >= 4 * I
                jj = j - 4 * I if diag else 0
                c0 = 128 * jj
                ps = g_rot.next()
                mm(ps, ps.ap[:, c0:512], [kz[m]], kz[m].ap[:, j * 128:(j + 1) * 128],
                   [qT], qT.ap[:, q0 + c0:q0 + 512], True, not diag)
                if diag:
                    mm(ps, ps.ap[:, c0:c0 + 128], [ident_b], ident_b.ap, [negm_b], negm_b.ap, False, True)
                tstate[t] = ps

            def st_pv(t):
                I, m, j = t
                q0 = 512 * I
                nkb = 4 * (I + 1)
                diag = j >= 4 * I
                jj = j - 4 * I if diag else 0
                c0 = 128 * jj
                ps = tstate.pop(t)
                ET = ET_rot.next()
                if wide:
                    col = h * 16 + (4 * I - j + 3)
                    act(ET, ET.ap[:, c0:512], [ps, ali2], ps.ap[:, c0:512], AF.Exp,
                        bias=ali2.ap[:, col:col + 1], scale=0.125)
                elif h == 1:
                    for i2 in range(jj // 2, 2):
                        lo = max(jj, 2 * i2)
                        r = 4 * I + (2 * i2 + 1) - j
                        col = h * 16 + r
                        act(ET, ET.ap[:, lo * 128:(2 * i2 + 2) * 128], [ps, alia], ps.ap[:, lo * 128:(2 * i2 + 2) * 128],
                            AF.Exp, bias=alia.ap[:, col:col + 1], scale=0.125)
                else:
                    for i in range(jj, 4):
                        r = 4 * I + i - j
                        col = h * 16 + r
                        act(ET, ET.ap[:, i * 128:(i + 1) * 128], [ps, alia], ps.ap[:, i * 128:(i + 1) * 128],
                            AF.Exp, bias=alia.ap[:, col:col + 1], scale=0.125)
                mm(psO[m], psO[m].ap[:, c0:512], [vtok], vtok.ap[:, j, :], [ET], ET.ap[:, c0:512],
                   j == 0, j == nkb - 1)
                mm(psZ[m], psZ[m].ap[:, c0:512], [ones_b], ones_b.ap, [ET], ET.ap[:, c0:512],
                   j == 0, j == nkb - 1)
                if m == 1 and j == nkb - 1:
                    r0, r1, t0_, t1_ = nrm
                    cp(r0, r0.ap, [psZ[0]], psZ[0].ap, eng="act")
                    cp(r1, r1.ap, [psZ[1]], psZ[1].ap, eng="act")
                    cp(t0_, t0_.ap, [psO[0]], psO[0].ap, eng="act")
                    cp(t1_, t1_.ap, [psO[1]], psO[1].ap, eng="act")

                    def s1():
                        recip(r0, r0.ap, [r0], r0.ap)
                    def s2():
                        recip(r1, r1.ap, [r1], r1.ap)
                    def s3():
                        tt(t0_, t0_.ap, [t0_, r0], t0_.ap, r0.ap, ALU.mult)
                        tt(t1_, t1_.ap, [t1_, r1], t1_.ap, r1.ap, ALU.mult)
                    def s4():
                        stt(t0_, t0_.ap, [t1_, nlam, t0_], t1_.ap, nlam.ap[:, 0:1], t0_.ap, ALU.mult, ALU.add)
                    def s5():
                        act(sqa, sqa.ap, [t0_], t0_.ap, AF.Square)
                    def s6():
                        psn = g_rot.next()
                        mm(psn, psn.ap, [ones_b], ones_b.ap, [sqa], sqa.ap, True, True)
                        act(r0, r0.ap, [psn, epsb], psn.ap, AF.Ln, bias=epsb.ap, scale=1.0 / 128)
                    def s7():
                        act(r0, r0.ap, [r0], r0.ap, AF.Exp, scale=-0.5)
                    def s8(q0=q0):
                        ao = ao_rot.next()
                        stt(ao, ao.ap, [t0_, sg08, r0], t0_.ap, sg08.ap[:, 0:1], r0.ap, ALU.mult, ALU.mult)
                        dma("sp", aoT_t, aoT_d[h * 128:(h + 1) * 128, q0:q0 + 512], ao, ao.ap)
                    pending.extend([s1, s2, s3, s4, s5, s6, s7, s8])

            pending = []
            LOOK = 2
            for idx in range(len(tasks) + LOOK):
                if idx < len(tasks):
                    st_score(tasks[idx])
                if idx >= LOOK:
                    st_pv(tasks[idx - LOOK])
                    if pending and tasks[idx - LOOK][2] >= 1:
                        pending.pop(0)()
            while pending:
                pending.pop(0)()
        release(mA)
        if stop_after == "attn":
            release(mU)
            return

        mT = mark()
        YNO = alloc([8192], F32, "YNO")
        YN = YNO.ap.bitcast(BF16).rearrange("p (a b) -> p a b", a=16)
        O = YNO.ap.rearrange("p (a b) -> p a b", a=8)
        AO = alloc([8, 1024], BF16, "AO")
        MG = alloc([8, 1024], BF16, "MG")
        wt_rot = Rot([alloc([40, 128], BF16, "wt%d" % i) for i in range(2)])
        wo_rot = Rot([alloc([8, 128], BF16, "wo%d" % i) for i in range(2)])
        gg = [alloc([512], F32, "gg%d" % i) for i in range(4)]
        rstd_rot = Rot([alloc([512], F32, "rstd%d" % i) for i in range(2)])
        sq_rot = Rot([alloc([512], BF16, "sq%d" % i) for i in range(2)])
        tmp_rot = Rot([gg[2], gg[3]])
        set_rot = Rot([PS[0:4], PS[4:8]])
        one_rot = Rot(PS)
        ynT_v2 = ynT_d.rearrange("(c p) t -> p c t", p=128)
        aoT_v2 = aoT_d.rearrange("(c p) t -> p c t", p=128)
        for half in range(2):
            hs = slice(half * 1024, (half + 1) * 1024)
            dma("sp", YNO, YN, ynT_t, ynT_v2[:, :, hs])
            dma("sp", AO, AO.ap, aoT_t, aoT_v2[:, :, hs])
            for d in range(8):
                wt = wt_rot.next()
                dma("pool", wt, wt.ap, None, wtail_d[d])
                for tb in range(2):
                    cs = slice(tb * 512, (tb + 1) * 512)
                    t0 = half * 1024 + tb * 512
                    pg0, pg1, pys, pya = set_rot.next()
                    for k in range(8):
                        mm(pg0, pg0.ap, [wt], wt.ap[:, k, :], [U], U.ap[:, k, t0:t0 + 512], k == 0, k == 7)
                    for k in range(8):
                        mm(pg1, pg1.ap, [wt], wt.ap[:, 8 + k, :], [U], U.ap[:, k, t0:t0 + 512], k == 0, k == 7)
                    for c in range(16):
                        mm(pys, pys.ap, [wt], wt.ap[:, 16 + c, :], [YNO], YN[:, c, cs], c == 0, c == 15)
                    for c in range(8):
                        mm(pya, pya.ap, [wt], wt.ap[:, 32 + c, :], [AO], AO.ap[:, c, cs], c == 0, c == 7)
                    i0 = (tb % 2) * 2
                    g0, g1 = gg[i0], gg[i0 + 1]
                    act(g0, g0.ap, [pg0, gateb], pg0.ap, AF.Sigmoid, bias=gateb.ap[:, d:d + 1], scale=1.0)
                    act(g1, g1.ap, [pg1, gateb], pg1.ap, AF.Sigmoid, bias=gateb.ap[:, 8 + d:9 + d], scale=1.0)
                    tt(g0, g0.ap, [g0, pys], g0.ap, pys.ap, ALU.mult)
                    tt(g1, g1.ap, [g1, pya], g1.ap, pya.ap, ALU.mult)
                    tt(MG, MG.ap[:, d, cs], [g0, g1], g0.ap, g1.ap, ALU.add)
            for d in range(8):
                wo = wo_rot.next()
                dma("pool", wo, wo.ap, None, wout_d[d])
                for tb in range(2):
                    cs = slice(tb * 512, (tb + 1) * 512)
                    ps = one_rot.next()
                    for k in range(8):
                        mm(ps, ps.ap, [wo], wo.ap[:, k, :], [MG], MG.ap[:, k, cs], k == 0, k == 7)
                    cp(YNO, O[:, d, cs], [ps], ps.ap, eng="act")
            for tb in range(2):
                t0 = half * 1024 + tb * 512
                cs = slice(tb * 512, (tb + 1) * 512)
                rstd = rstd_rot.next()
                rms_stats([YNO], [O[:, c, cs] for c in range(8)], D, one_rot.next(), sq_rot, rstd, rstd.ap)
                for c in range(8):
                    tmp = tmp_rot.next()
                    stt(tmp, tmp.ap, [YNO, gains, rstd], O[:, c, cs], gains.ap[:, 3, c:c + 1], rstd.ap,
                        ALU.mult, ALU.mult)
                    tt(H, H.ap[:, c, t0:t0 + 512], [H, tmp], H.ap[:, c, t0:t0 + 512], tmp.ap, ALU.add,
                       eng=("pool" if c % 3 == 2 else "dve"))
        release(mT)
        release(mU)

    if stop_after != "ffn1":
        mixer()
        dump_H("h2T")
    outT_v = outT_d.rearrange("(c p) t -> p c t", p=128)

    def store_half0():
        t_ = T(outT_d, "outT_h0")
        out_ops.append(dma("sp", t_, outT_v[:, :, 0:1024], H, H.ap[:, :, 0:1024]))

    if stop_after is None:
        ffn("ffn2", 4, 5, out_hook=store_half0)
        lo = 1024
    else:
        lo = 0

    for c in range(8):
        t_ = T(outT_d, "outT%d" % c)
        out_ops.append(dma("sp", t_, outT_d[c * 128:(c + 1) * 128, lo:L], H, H.ap[:, c, lo:L]))
    S.add("sp", lambda e: None, extra=out_ops + [op for op in S.dma_ops[-N_DMA_SEMS:]])
    S.emit()
    return nc, S


def _colblk(W):
    K, N = W.shape
    return np.ascontiguousarray(W.reshape(K // 128, 128, N // 128, 128).transpose(2, 1, 0, 3))


def _rowmaj(W):
    K, N = W.shape
    return np.ascontiguousarray(W.reshape(K // 128, 128, N).transpose(1, 0, 2))


def _fm(v):
    return np.ascontiguousarray(v.reshape(-1, 128).T)


def _bc(v):
    return np.ascontiguousarray(np.broadcast_to(v[None, :], (128, v.shape[0])))


def make_shared(inp):
    f = lambda a: np.asarray(a, dtype=np.float32)
    sh = {}
    for pre in ("ffn1", "ffn2"):
        sh[pre + "_wg"] = _colblk(f(inp[pre + "_w_gate"])[0])
        sh[pre + "_wu"] = _colblk(f(inp[pre + "_w_up"])[0])
        sh[pre + "_wd"] = _colblk(f(inp[pre + "_w_down"])[0])
    w_in = f(inp["w_in"])[0]
    z = w_in[:, 0:2048]
    xbc = w_in[:, 2048:5120]
    dtw = w_in[:, 5120:5152]
    q = w_in[:, 5152:6176]
    k = w_in[:, 6176:7200]
    v = w_in[:, 7200:8224]
    gate = w_in[:, 8224:10272]
    grp = []
    for g in range(4):
        cols = np.concatenate([z[:, g * 512:(g + 1) * 512], xbc[:, g * 512:(g + 1) * 512],
                               xbc[:, 2048 + g * 128:2048 + (g + 1) * 128],
                               xbc[:, 2560 + g * 128:2560 + (g + 1) * 128]], axis=1)
        grp.append(_rowmaj(cols))
    sh["w_grp"] = np.stack(grp)
    sh["w_dt"] = _rowmaj(dtw)
    qb, kb, vb = _colblk(q), _colblk(k), _colblk(v)
    sh["w_head"] = np.ascontiguousarray(np.stack([qb, kb, vb], axis=2))
    gb = _colblk(gate)
    wbs = _colblk(f(inp["ssd_w_branch"])[0])
    wbd = _colblk(f(inp["da_w_branch"])[0])
    sh["w_tail"] = np.ascontiguousarray(np.concatenate([gb[0:8], gb[8:16], wbs, wbd], axis=2))
    sh["w_outb"] = _colblk(f(inp["w_out"])[0])
    gl = [inp[n] for n in ("ffn1_pre_g", "ffn1_post_g", "mix_pre_g", "mix_post_g", "ffn2_pre_g", "ffn2_post_g")]
    sh["gains"] = np.ascontiguousarray(np.stack([_fm(f(g_)[0]) for g_ in gl], axis=1))
    sh["gate_b"] = _fm(f(inp["gate_b"])[0])
    cw = f(inp["ssd_conv_w"])[0]
    sh["conv_w"] = np.ascontiguousarray(cw.reshape(4, 24, 128).transpose(2, 1, 0))
    sh["conv_b"] = _fm(f(inp["ssd_conv_b"])[0])
    sh["headvec"] = np.ascontiguousarray(np.stack([_bc(f(inp["ssd_dt_bias"])[0]), _bc(f(inp["ssd_A_log"])[0]),
                                                   _bc(f(inp["ssd_D"])[0])], axis=1))
    sh["ssd_norm_g"] = _bc(f(inp["ssd_norm_g"])[0])
    sh["subln_g"] = np.ascontiguousarray(f(inp["da_subln_g"])[0].reshape(128, 1))
    sh["lamv"] = np.ascontiguousarray(np.stack([_bc(f(inp[n])[0]) for n in
                                                ("da_lambda_q1", "da_lambda_k1", "da_lambda_q2", "da_lambda_k2")], axis=1))
    p = np.arange(128, dtype=np.float32)
    ident = np.eye(128, dtype=np.float32)
    tri = (p[:, None] <= p[None, :]).astype(np.float32)
    negm = np.where(p[None, :] >= p[:, None], 0.0, NEG).astype(np.float32)
    slopes = 2.0 ** (-(np.arange(8, dtype=np.float32) + 1.0))
    r = np.arange(16, dtype=np.float32)
    ali = (slopes[None, :, None] * (p[:, None, None] - 128.0 * r[None, None, :] - 127.0)).reshape(128, 128)
    ali2 = (slopes[None, :, None] * (p[:, None, None] - 128.0 * (r[None, None, :] - 3.0) - 511.0)).reshape(128, 128)
    sh["consts"] = np.ascontiguousarray(np.stack([ident, tri, negm, ali.astype(np.float32)], axis=1))
    sh["alibi2"] = np.ascontiguousarray(ali2.astype(np.float32))
    return sh


_CACHE = {}


def kernel(**inputs):
    x = np.asarray(inputs["x"], dtype=np.float32)
    sh = make_shared(inputs)
    if "nc" not in _CACHE:
        _CACHE["nc"] = build_program()[0]
    nc = _CACHE["nc"]
    in_maps = []
    for b in range(8):
        m = dict(sh)
        m["xT"] = np.ascontiguousarray(x[b].T)
        in_maps.append(m)
    res = run_bass_kernel_spmd(nc, in_maps, core_ids=list(range(8)))
    out = np.stack([np.ascontiguousarray(res.results[b]["outT"].T) for b in range(8)], axis=0)
    return out.astype(np.float32)
```

```python
import numpy as np
import concourse.bass as bass
import concourse.mybir as mybir
from concourse.bass_utils import run_bass_kernel_spmd

F32 = mybir.dt.float32
BF16 = mybir.dt.bfloat16
AF = mybir.ActivationFunctionType
ALU = mybir.AluOpType
AX = mybir.AxisListType

D = 1024
L = 2048
DFF = 2816
NF = DFF // 128
EPS = 1e-6
N_DMA_SEMS = 24
NEG = -30000.0
DBG = {"ng": 4, "ntb": 4, "stage": 99, "skip_ffn1": 0, "sub": 99}


class Buf:
    __slots__ = ("name", "last_w", "readers", "excl")

    def __init__(self, name):
        self.name = name
        self.last_w = None
        self.readers = []
        self.excl = False


class Op:
    __slots__ = ("eng", "fn", "is_dma", "deps", "signal", "sem", "val", "clock", "idx", "inc", "own", "key", "clear")

    def __init__(self, eng, fn, is_dma):
        self.eng = eng
        self.fn = fn
        self.is_dma = is_dma
        self.deps = []
        self.signal = False
        self.sem = None
        self.val = 0
        self.clock = None
        self.inc = 1
        self.own = None
        self.key = None
        self.clear = False


class Sched:
    ENGINES = ("pe", "act", "dve", "pool", "sp")

    def __init__(self, nc):
        self.nc = nc
        self.ops = []
        self.dma_ops = []

    def add(self, eng, fn, reads=(), writes=(), is_dma=False, extra=(), own=None):
        op = Op(eng, fn, is_dma)
        op.idx = len(self.ops)
        op.own = own
        deps = {}

        def dep(p, raw):
            if p is None:
                return
            if (not p.is_dma) and (not is_dma) and p.eng == eng:
                if eng == "pe" or not raw:
                    return
            deps[p.idx] = p

        reads = list(reads)
        writes = list(writes)
        for b in list(reads):
            if b.excl:
                reads.remove(b)
                if b not in writes:
                    writes.append(b)
        for b in reads:
            dep(b.last_w, True)
        for b in writes:
            dep(b.last_w, False)
            for r in b.readers:
                dep(r, False)
        for p in extra:
            deps[p.idx] = p
        for b in reads:
            b.readers.append(op)
        for b in writes:
            b.last_w = op
            b.readers = []
        if is_dma and own is None:
            n = len(self.dma_ops)
            if n >= N_DMA_SEMS:
                p = self.dma_ops[n - N_DMA_SEMS]
                deps[p.idx] = p
            self.dma_ops.append(op)
            op.signal = True
        elif is_dma:
            op.signal = True
        op.deps = list(deps.values())
        for p in op.deps:
            p.signal = True
        self.ops.append(op)
        return op

    def emit(self):
        nc = self.nc
        esem = {e: nc.alloc_semaphore("s_" + e) for e in self.ENGINES}
        dsem = [nc.alloc_semaphore("s_dma%d" % i) for i in range(N_DMA_SEMS)]
        cnt = {e: 0 for e in self.ENGINES}
        ndma = 0
        own_sems = {}
        for op in self.ops:
            if op.is_dma and op.own is not None:
                if DBG.get("fresh"):
                    own_sems[id(op.own)] = [nc.alloc_semaphore("s_f%d" % op.idx), 0]
                if id(op.own) not in own_sems:
                    own_sems[id(op.own)] = [nc.alloc_semaphore("s_slot%d" % len(own_sems)), 0]
                ent = own_sems[id(op.own)]
                ent[1] += 1
                op.sem = ent[0]
                op.val = 16 * ent[1]
                op.inc = 16
                op.key = id(op.sem)
                op.clear = False
            elif op.is_dma:
                op.sem = dsem[ndma % N_DMA_SEMS]
                op.val = 16 * (ndma // N_DMA_SEMS + 1)
                op.inc = 16
                op.key = id(op.sem)
                ndma += 1
            elif op.signal:
                cnt[op.eng] += 1
                op.sem = esem[op.eng]
                op.val = cnt[op.eng]
                op.key = id(op.sem)
        known = {e: {} for e in self.ENGINES}
        waits = {}
        for op in self.ops:
            k = known[op.eng]
            w = []
            for p in sorted(op.deps, key=lambda p: -p.idx):
                sid = p.key
                if k.get(sid, (None, 0))[1] >= p.val:
                    continue
                w.append((p.sem, p.val))
                for s2, v2 in p.clock.items():
                    if k.get(s2, (None, 0))[1] < v2[1]:
                        k[s2] = v2
            waits[op.idx] = w
            if op.signal:
                c = dict(k)
                c[op.key] = (op.sem, op.val)
                op.clock = c
        per_eng = {e: [op for op in self.ops if op.eng == e] for e in self.ENGINES}
        self.stats = {e: len(per_eng[e]) for e in self.ENGINES}
        self.stats["waits"] = sum(len(w) for w in waits.values())

        def make(e):
            def body(eng):
                for op in per_eng[e]:
                    for (s, v) in waits[op.idx]:
                        eng.wait_ge(s, v)
                    if op.clear:
                        if op.key[1] > 1:
                            eng.wait_ge(op.sem, 16)
                        eng.sem_clear(op.sem)
                    ins = op.fn(eng)
                    if op.signal:
                        ins.then_inc(op.sem, op.inc)
            return body

        with nc.Block() as block:
            block.tensor(make("pe"))
            block.scalar(make("act"))
            block.vector(make("dve"))
            block.gpsimd(make("pool"))
            block.sync(make("sp"))


class T:
    __slots__ = ("ap", "buf")

    def __init__(self, ap, name):
        self.ap = ap
        self.buf = Buf(name)


def bcast(ap, shape, axis):
    return ap.unsqueeze(axis).broadcast_to(list(shape))


def build_program(debug=False, stop_after=None):
    nc = bass.Bass("TRN2", target_bir_lowering=False)
    S = Sched(nc)

    def din(name, shape):
        return nc.dram_tensor(name, list(shape), F32, kind="ExternalInput").ap()

    xT_d = din("xT", [D, L])
    w_d = {}
    for pre in ("ffn1", "ffn2"):
        w_d[pre + "_wg"] = din(pre + "_wg", [NF, 128, 8, 128])
        w_d[pre + "_wu"] = din(pre + "_wu", [NF, 128, 8, 128])
        w_d[pre + "_wd"] = din(pre + "_wd", [8, 128, NF, 128])
    wgrp_d = din("w_grp", [4, 128, 8, 1280])
    wdt_d = din("w_dt", [128, 8, 32])
    whead_d = din("w_head", [8, 128, 3, 8, 128])
    wtail_d = din("w_tail", [8, 128, 40, 128])
    wout_d = din("w_outb", [8, 128, 8, 128])
    gains_d = din("gains", [128, 6, 8])
    gateb_d = din("gate_b", [128, 16])
    convw_d = din("conv_w", [128, 24, 4])
    convb_d = din("conv_b", [128, 24])
    hv_d = din("headvec", [128, 3, 32])
    normg_d = din("ssd_norm_g", [128, 2048])
    subln_d = din("subln_g", [128, 1])
    lamv_d = din("lamv", [128, 4, 64])
    cst_d = din("consts", [128, 4, 128])
    ali2_d = din("alibi2", [128, 128])
    outT_d = nc.dram_tensor("outT", [D, L], F32, kind="ExternalOutput").ap()
    skind = "ExternalOutput" if debug else "Internal"
    ynT_d = nc.dram_tensor("ynT", [2048, L], BF16, kind=skind).ap()
    aoT_d = nc.dram_tensor("aoT", [1024, L], BF16, kind=skind).ap()
    ynT_t = T(ynT_d, "ynT_d")
    aoT_t = T(aoT_d, "aoT_d")
    dbg_d = {}
    if debug:
        for nm in ("h1T", "h2T"):
            dbg_d[nm] = nc.dram_tensor(nm, [D, L], F32, kind="ExternalOutput").ap()
    out_ops = []

    NW = 52800
    arena = nc.alloc_sbuf_tensor("arena", [128, NW], F32)
    top = [0]
    alloc_log = []

    def alloc(shape, dt, name):
        n = int(np.prod(shape))
        nb = n * (4 if dt == F32 else 2)
        nb = (nb + 31) // 32 * 32
        off = top[0]
        top[0] += nb
        assert top[0] <= NW * 4, ("SBUF arena overflow", name, top[0])
        inherit = []
        for (s2, e2, t2) in alloc_log:
            if s2 < off + nb and off < e2:
                inherit.extend(t2.buf.readers)
                if t2.buf.last_w is not None:
                    inherit.append(t2.buf.last_w)
        a = arena[:, off // 4: off // 4 + nb // 4]
        if dt == BF16:
            a = a.bitcast(BF16)
        a = a[:, 0:n]
        if len(shape) == 2:
            a = a.rearrange("p (a b) -> p a b", a=shape[0])
        elif len(shape) == 3:
            a = a.rearrange("p (a b c) -> p a b c", a=shape[0], b=shape[1])
        t = T(a, name)
        t.buf.readers = inherit
        alloc_log.append((off, off + nb, t))
        return t

    def mark():
        return top[0]

    def release(m):
        top[0] = m

    PS = []
    for i in range(8):
        p = nc.alloc_psum_tensor("ps%d" % i, [128, 512], F32)
        PS.append(T(p[:], "ps%d" % i))
        PS[-1].buf.excl = True

    class Rot:
        def __init__(self, items):
            self.items = items
            self.i = 0

        def next(self):
            it = self.items[self.i % len(self.items)]
            self.i += 1
            return it

    def dma(q, out_t, out_ap, in_t, in_ap):
        return S.add(q, lambda e: e.dma_start(out=out_ap, in_=in_ap),
                     reads=[in_t.buf] if in_t is not None else [],
                     writes=[out_t.buf], is_dma=True, own=(out_t if q == "pool" else None))

    def mm(out_t, out_ap, l_ts, lhsT, r_ts, rhs, start, stop):
        S.add("pe", lambda e: e.matmul(out_ap, lhsT=lhsT, rhs=rhs, start=start, stop=stop),
              reads=[t.buf for t in l_ts] + [t.buf for t in r_ts], writes=[out_t.buf])

    def tr(out_t, out_ap, in_t, in_ap, ident_t, ident_ap):
        S.add("pe", lambda e: e.transpose(out_ap, in_ap, ident_ap),
              reads=[in_t.buf, ident_t.buf], writes=[out_t.buf])

    def act(out_t, out_ap, in_ts, in_ap, func, bias=None, scale=None, accum=None, extra_w=()):
        def f(e):
            kw = {}
            if bias is not None:
                kw["bias"] = bias
            if scale is not None:
                kw["scale"] = scale
            if accum is not None:
                kw["accum_out"] = accum
            return e.activation(out=out_ap, in_=in_ap, func=func, **kw)
        S.add("act", f, reads=[t.buf for t in in_ts], writes=[out_t.buf] + [t.buf for t in extra_w])

    def tt(out_t, out_ap, in_ts, in0, in1, op, eng="dve"):
        S.add(eng, lambda e: e.tensor_tensor(out=out_ap, in0=in0, in1=in1, op=op),
              reads=[t.buf for t in in_ts], writes=[out_t.buf])

    def stt(out_t, out_ap, in_ts, in0, scalar, in1, op0, op1, eng="dve"):
        S.add(eng, lambda e: e.scalar_tensor_tensor(out=out_ap, in0=in0, scalar=scalar, in1=in1, op0=op0, op1=op1),
              reads=[t.buf for t in in_ts], writes=[out_t.buf])

    def ts(out_t, out_ap, in_ts, in0, s1, s2, op0, op1=None, eng="dve"):
        def f(e):
            if op1 is None:
                return e.tensor_scalar(out=out_ap, in0=in0, scalar1=s1, scalar2=None, op0=op0)
            return e.tensor_scalar(out=out_ap, in0=in0, scalar1=s1, scalar2=s2, op0=op0, op1=op1)
        S.add(eng, f, reads=[t.buf for t in in_ts], writes=[out_t.buf])

    def cp(out_t, out_ap, in_ts, in_ap, eng="dve"):
        if eng == "act":
            act(out_t, out_ap, in_ts, in_ap, AF.Copy)
        else:
            S.add(eng, lambda e: e.tensor_copy(out=out_ap, in_=in_ap),
                  reads=[t.buf for t in in_ts], writes=[out_t.buf])

    def recip(out_t, out_ap, in_ts, in_ap):
        S.add("dve", lambda e: e.reciprocal(out=out_ap, in_=in_ap),
              reads=[t.buf for t in in_ts], writes=[out_t.buf])

    def memset(out_t, out_ap, val):
        S.add("dve", lambda e: e.memset(out_ap, val), writes=[out_t.buf])

    H = alloc([8, L], F32, "H")
    gains = alloc([6, 8], F32, "gains")
    gainsh = alloc([6, 8], F32, "gainsh")
    gateb = alloc([16], F32, "gateb")
    convw = alloc([24, 4], F32, "convw")
    convb = alloc([24], F32, "convb")
    hv = alloc([3, 32], F32, "hv")
    Aneg = alloc([32], F32, "Aneg")
    subln = alloc([1], F32, "subln")
    sg08 = alloc([1], F32, "sg08")
    lamv = alloc([4, 64], F32, "lamv")
    lamt = alloc([2, 64], F32, "lamt")
    lams = alloc([2], F32, "lams")
    lame = alloc([2], F32, "lame")
    nlam = alloc([1], F32, "nlam")
    cst = alloc([4, 128], F32, "cst")
    ali2 = alloc([128], F32, "ali2")
    ident_f = T(cst.ap[:, 0, :], "ident_f"); ident_f.buf = cst.buf
    tri_f = T(cst.ap[:, 1, :], "tri_f"); tri_f.buf = cst.buf
    ident_b = alloc([128], BF16, "ident_b")
    negm_b = alloc([128], BF16, "negm_b")
    ones_b = alloc([128], BF16, "ones_b")
    tri_b = alloc([128], BF16, "tri_b")
    epsb = alloc([1], F32, "epsb")
    onef = alloc([1], F32, "onef")

    dma("sp", gains, gains.ap, None, gains_d)
    dma("sp", gateb, gateb.ap, None, gateb_d)
    dma("sp", convw, convw.ap, None, convw_d)
    dma("sp", convb, convb.ap, None, convb_d)
    dma("sp", hv, hv.ap, None, hv_d)
    dma("sp", subln, subln.ap, None, subln_d)
    dma("sp", lamv, lamv.ap, None, lamv_d)
    dma("sp", cst, cst.ap, None, cst_d)
    dma("sp", ali2, ali2.ap, None, ali2_d)
    xT_v = xT_d.rearrange("(c p) t -> p c t", p=128)
    H_ld = []
    for tb in range(4):
        t_ = T(H.ap, "Hld%d" % tb)
        dma("sp", t_, H.ap[:, :, tb * 512:(tb + 1) * 512], None, xT_v[:, :, tb * 512:(tb + 1) * 512])
        H_ld.append(t_)
    ts(gainsh, gainsh.ap, [gains], gains.ap, 0.5, None, ALU.mult)
    cp(ident_b, ident_b.ap, [cst], cst.ap[:, 0, :])
    cp(negm_b, negm_b.ap, [cst], cst.ap[:, 2, :])
    cp(tri_b, tri_b.ap, [cst], cst.ap[:, 1, :])
    memset(ones_b, ones_b.ap, 1.0)
    memset(epsb, epsb.ap, EPS)
    memset(onef, onef.ap, 1.0)
    act(Aneg, Aneg.ap, [hv], hv.ap[:, 1, :], AF.Exp)
    ts(Aneg, Aneg.ap, [Aneg], Aneg.ap, -1.0, None, ALU.mult)
    ts(sg08, sg08.ap, [subln], subln.ap, 0.8, None, ALU.mult)
    tt(lamt, lamt.ap[:, 0, :], [lamv], lamv.ap[:, 0, :], lamv.ap[:, 1, :], ALU.mult)
    tt(lamt, lamt.ap[:, 1, :], [lamv], lamv.ap[:, 2, :], lamv.ap[:, 3, :], ALU.mult)
    S.add("dve", lambda e: e.tensor_reduce(out=lams.ap, in_=lamt.ap, axis=AX.X, op=ALU.add),
          reads=[lamt.buf], writes=[lams.buf])
    act(lame, lame.ap, [lams], lams.ap, AF.Exp)
    tt(nlam, nlam.ap, [lame], lame.ap[:, 1:2], lame.ap[:, 0:1], ALU.subtract)
    ts(nlam, nlam.ap, [nlam], nlam.ap, -0.2, None, ALU.add)

    alia = T(cst.ap[:, 3, :], "alia"); alia.buf = cst.buf

    def rms_stats(src_ts, src_aps, nfeat, ps_t, sq_rot, rstd_t, rstd_ap):
        n = len(src_aps)
        W = src_aps[0].shape[-1]
        for c in range(n):
            sq = sq_rot.next()
            act(sq, sq.ap[:, 0:W], src_ts, src_aps[c], AF.Square)
            mm(ps_t, ps_t.ap[:, 0:W], [ones_b], ones_b.ap, [sq], sq.ap[:, 0:W], c == 0, c == n - 1)
        act(rstd_t, rstd_ap, [ps_t, epsb], ps_t.ap[:, 0:W], AF.Ln, bias=epsb.ap, scale=1.0 / nfeat)
        act(rstd_t, rstd_ap, [rstd_t], rstd_ap, AF.Exp, scale=-0.5)

    def dump_H(name):
        if debug:
            for c in range(8):
                t_ = T(dbg_d[name], name + str(c))
                op = dma("sp", t_, dbg_d[name][c * 128:(c + 1) * 128, :], H, H.ap[:, c, :])
                out_ops.append(op)

    def ffn(pre, gi_pre, gi_post, ld=None, out_hook=None):
        m0 = mark()
        XN = alloc([8, 1024], BF16, "XN")
        ACTB = alloc([NF, 1024], BF16, "ACTB")
        Y = alloc([8, 1024], F32, "Y")
        wslots = Rot([alloc([NF, 128], BF16, "wslot%d" % i) for i in range(4)])
        rstd_rot = Rot([alloc([512], F32, "rstd%d" % i) for i in range(2)])
        sq_rot = Rot([alloc([512], BF16, "sq%d" % i) for i in range(2)])
        sil_rot = Rot([alloc([512], F32, "sil%d" % i) for i in range(2)])
        tmp_rot = Rot([alloc([512], F32, "tmp%d" % i) for i in range(3)])
        up_rot = Rot([(PS[0], PS[1]), (PS[2], PS[3])])
        dn_rot = Rot([PS[4], PS[5]])
        st_rot = Rot([PS[6], PS[7]])
        wg_d, wu_d, wd_d = w_d[pre + "_wg"], w_d[pre + "_wu"], w_d[pre + "_wd"]
        def pre_norm(half):
            for tb in range(2):
                t0 = half * 1024 + tb * 512
                rstd = rstd_rot.next()
                hrd = [H] + ([ld[half * 2 + tb]] if ld is not None else [])
                rms_stats(hrd, [H.ap[:, c, t0:t0 + 512] for c in range(8)], D, st_rot.next(), sq_rot, rstd, rstd.ap)
                for c in range(8):
                    stt(XN, XN.ap[:, c, tb * 512:(tb + 1) * 512], hrd + [gains, rstd], H.ap[:, c, t0:t0 + 512],
                        gains.ap[:, gi_pre, c:c + 1], rstd.ap, ALU.mult, ALU.mult)

        def up(hook):
            for f in range(NF):
                wg = wslots.next()
                dma("pool", wg, wg.ap[:, 0:8, :], None, wg_d[f])
                wu = wslots.next()
                dma("pool", wu, wu.ap[:, 0:8, :], None, wu_d[f])
                for tb in range(2):
                    pg, pu = up_rot.next()
                    cs = slice(tb * 512, (tb + 1) * 512)
                    for k in range(8):
                        mm(pg, pg.ap, [wg], wg.ap[:, k, :], [XN], XN.ap[:, k, cs], k == 0, k == 7)
                    for k in range(8):
                        mm(pu, pu.ap, [wu], wu.ap[:, k, :], [XN], XN.ap[:, k, cs], k == 0, k == 7)
                    sil = sil_rot.next()
                    act(sil, sil.ap, [pg], pg.ap, AF.Silu)
                    tt(ACTB, ACTB.ap[:, f, cs], [sil, pu], sil.ap, pu.ap, ALU.mult)
                if hook:
                    hook.pop(0)()

        def down():
            stb = [st_rot.next(), st_rot.next()]
            pend = []
            for d in range(8):
                wd = wslots.next()
                dma("pool", wd, wd.ap, None, wd_d[d])
                for tb in range(2):
                    ps = dn_rot.next()
                    cs = slice(tb * 512, (tb + 1) * 512)
                    for f in range(NF):
                        mm(ps, ps.ap, [wd], wd.ap[:, f, :], [ACTB], ACTB.ap[:, f, cs], f == 0, f == NF - 1)
                    while pend:
                        ptb, psq, pd = pend.pop(0)
                        mm(stb[ptb], stb[ptb].ap, [ones_b], ones_b.ap, [psq], psq.ap, pd == 0, pd == 7)
                    cp(Y, Y.ap[:, d, cs], [ps], ps.ap, eng="act")
                    sq = sq_rot.next()
                    act(sq, sq.ap, [ps], ps.ap, AF.Square)
                    pend.append((tb, sq, d))
            while pend:
                ptb, psq, pd = pend.pop(0)
                mm(stb[ptb], stb[ptb].ap, [ones_b], ones_b.ap, [psq], psq.ap, pd == 0, pd == 7)
            return stb

        def post_units(half, stb):
            units = []
            for tb in range(2):
                t0 = half * 1024 + tb * 512
                cs = slice(tb * 512, (tb + 1) * 512)
                rstd = rstd_rot.next()
                act(rstd, rstd.ap, [stb[tb], epsb], stb[tb].ap, AF.Ln, bias=epsb.ap, scale=1.0 / D)
                act(rstd, rstd.ap, [rstd], rstd.ap, AF.Exp, scale=-0.5)
                for c in range(8):
                    def unit(c=c, t0=t0, cs=cs, rstd=rstd):
                        tmp = tmp_rot.next()
                        stt(tmp, tmp.ap, [Y, gainsh, rstd], Y.ap[:, c, cs], gainsh.ap[:, gi_post, c:c + 1], rstd.ap,
                            ALU.mult, ALU.mult)
                        tt(H, H.ap[:, c, t0:t0 + 512], [H, tmp], H.ap[:, c, t0:t0 + 512], tmp.ap, ALU.add,
                           eng=("pool" if c % 3 == 2 else "dve"))
                    units.append(unit)
            return units

        pre_norm(0)
        up(None)
        pre_norm(1)
        stb0 = down()
        units = post_units(0, stb0)
        up(units)
        for u_ in units[:]:
            units.pop(0)()
        if out_hook is not None:
            out_hook()
        stb1 = down()
        for u_ in post_units(1, stb1):
            u_()
        release(m0)

    if not DBG["skip_ffn1"]:
        ffn("ffn1", 0, 1, ld=H_ld)
    dump_H("h1T")

    def mixer():
        mU = mark()
        U = alloc([8, L], BF16, "U")
        m1 = mark()
        rstd_rot = Rot([alloc([512], F32, "rstd%d" % i) for i in range(2)])
        sq_rot = Rot([alloc([512], BF16, "sq%d" % i) for i in range(2)])
        st_rot = Rot([PS[6], PS[7]])
        for tb in range(4):
            t0 = tb * 512
            rstd = rstd_rot.next()
            rms_stats([H], [H.ap[:, c, t0:t0 + 512] for c in range(8)], D, st_rot.next(), sq_rot, rstd, rstd.ap)
            for c in range(8):
                stt(U, U.ap[:, c, t0:t0 + 512], [H, gains, rstd], H.ap[:, c, t0:t0 + 512],
                    gains.ap[:, 2, c:c + 1], rstd.ap, ALU.mult, ALU.mult)
        release(m1)

        mS = mark()
        wdt = alloc([8, 32], BF16, "wdt")
        dt_all = alloc([16, 32], F32, "dt_all")
        a_all = alloc([16, 32], F32, "a_all")
        a_hi = alloc([16, 32], BF16, "a_hi")
        a_lo = alloc([16, 32], BF16, "a_lo")
        na_hi = alloc([16, 32], BF16, "na_hi")
        na_lo = alloc([16, 32], BF16, "na_lo")
        mtmp = mark()
        tA = alloc([16, 32], F32, "tA")
        tB = alloc([16, 32], F32, "tB")
        dma("pool", wdt, wdt.ap, None, wdt_d)
        psd = PS[0]
        for tc in range(16):
            for k in range(8):
                mm(psd, psd.ap[:, tc * 32:(tc + 1) * 32], [U], U.ap[:, k, tc * 128:(tc + 1) * 128],
                   [wdt], wdt.ap[:, k, :], k == 0, k == 7)
        psd3 = psd.ap.rearrange("p (a b) -> p a b", a=16)
        tt(tA, tA.ap, [psd, hv], psd3, bcast(hv.ap[:, 0, :], [128, 16, 32], 1), ALU.add)
        stt(tB, tB.ap, [tA], tA.ap, -1.0, tA.ap, ALU.mult, ALU.max)
        act(tB, tB.ap, [tB], tB.ap, AF.Exp, scale=-1.0)
        act(tB, tB.ap, [tB, onef], tB.ap, AF.Ln, bias=onef.ap, scale=1.0)
        stt(dt_all, dt_all.ap, [tA, tB], tA.ap, 0.0, tB.ap, ALU.max, ALU.add)
        tt(a_all, a_all.ap, [dt_all, Aneg], dt_all.ap, bcast(Aneg.ap, [128, 16, 32], 1), ALU.mult)
        cp(a_hi, a_hi.ap, [a_all], a_all.ap)
        tt(tA, tA.ap, [a_all, a_hi], a_all.ap, a_hi.ap, ALU.subtract)
        cp(a_lo, a_lo.ap, [tA], tA.ap)
        ts(na_hi, na_hi.ap, [a_hi], a_hi.ap, -1.0, None, ALU.mult)
        ts(na_lo, na_lo.ap, [a_lo], a_lo.ap, -1.0, None, ALU.mult)
        release(mtmp)

        WG = alloc([8, 1280], BF16, "WG")
        normg = alloc([512], F32, "normg")
        wdiag = alloc([6, 4, 128], BF16, "wdiag")
        prevT = alloc([512], F32, "prevT")
        prevTb = alloc([512], BF16, "prevTb")
        Et = alloc([8, 128], F32, "E")

        class Bufs:
            pass
        blk = []
        for i in range(2):
            bb = Bufs()
            bb.xpre = [alloc([516], BF16, "xpre%d_%d" % (i, j)) for j in range(6)]
            bb.xsT = alloc([4, 512], BF16, "xsT%d" % i)
            bb.BT = alloc([512], BF16, "BT%d" % i)
            bb.CT = alloc([512], BF16, "CT%d" % i)
            bb.zs = alloc([4, 512], BF16, "zs%d" % i)
            blk.append(bb)
        cbs = []
        for i in range(2):
            cb = Bufs()
            cb.xs_tok = alloc([512], BF16, "xs_tok%d" % i)
            cb.Xdt = alloc([512], BF16, "Xdt%d" % i)
            cb.Xd = alloc([512], BF16, "Xd%d" % i)
            cb.Btok = alloc([128], BF16, "Btok%d" % i)
            cb.small = alloc([8, 8], F32, "small%d" % i)
            cb.MT = alloc([8, 128], BF16, "MT%d" % i)
            cb.yoff = alloc([512], F32, "yoff%d" % i)
            cb.dx = alloc([512], F32, "dx%d" % i)
            cb.y = alloc([512], F32, "y%d" % i)
            cb.yn = alloc([512], BF16, "yn%d" % i)
            cb.ynT = alloc([4, 128], BF16, "ynT%d" % i)
            cbs.append(cb)
        cb_rot = Rot(cbs)
        ip_rot = Rot([PS[0], PS[1]])
        psX, psM, psR0, psR1, psY, psYo = PS[2], PS[3], PS[4], PS[5], PS[6], PS[7]
        psT = psY
        ynT_v = ynT_d.rearrange("(c p) t -> p c t", p=128)
        v3 = lambda ap: ap.rearrange("p (a b) -> p a b", a=8)

        def group_fns(g):
            chs = [4 * g + j if j < 4 else (16 + g if j == 4 else 20 + g) for j in range(6)]

            def group_setup():
                dma("pool", WG, WG.ap, None, wgrp_d[g])
                for j in range(6):
                    for k in range(4):
                        ts(wdiag, wdiag.ap[:, j, k, :], [ident_b, convw], ident_b.ap, convw.ap[:, chs[j], k:k + 1], None,
                           ALU.mult)

            def group_begin():
                dma("sp", normg, normg.ap, None, normg_d[:, g * 512:(g + 1) * 512])
                memset(prevT, prevT.ap, 0.0)
                memset(prevTb, prevTb.ap, 0.0)

            def inproj(tb):
                bb = blk[tb % 2]
                pb = blk[(tb - 1) % 2]
                t0 = tb * 512
                st = {}

                def mm_a(j):
                    ps = ip_rot.next()
                    for k in range(8):
                        mm(ps, ps.ap, [WG], WG.ap[:, k, 512 + j * 128:512 + (j + 1) * 128],
                           [U], U.ap[:, k, t0:t0 + 512], k == 0, k == 7)
                    st[("a", j)] = ps

                def evac_a(j):
                    ps = st.pop(("a", j))
                    xp = bb.xpre[j]
                    cp(xp, xp.ap[:, 3:515], [ps], ps.ap, eng="act")
                    if tb == 0:
                        memset(xp, xp.ap[:, 0:3], 0.0)
                    else:
                        cp(xp, xp.ap[:, 0:3], [pb.xpre[j]], pb.xpre[j].ap[:, 512:515])

                def conv(j):
                    xp = bb.xpre[j]
                    ps2 = ip_rot.next()
                    for k in range(4):
                        mm(ps2, ps2.ap, [wdiag], wdiag.ap[:, j, k, :], [xp], xp.ap[:, k:k + 512], k == 0, k == 3)
                    st[("c", j)] = ps2

                def silu_c(j):
                    ps2 = st.pop(("c", j))
                    if j < 4:
                        dst, dap = bb.xsT, bb.xsT.ap[:, j, :]
                    elif j == 4:
                        dst, dap = bb.BT, bb.BT.ap
                    else:
                        dst, dap = bb.CT, bb.CT.ap
                    act(dst, dap, [ps2, convb], ps2.ap, AF.Silu, bias=convb.ap[:, chs[j]:chs[j] + 1], scale=1.0)

                def mm_z(cc):
                    ps = ip_rot.next()
                    l0 = t0 + cc * 128
                    for k in range(8):
                        mm(ps, ps.ap, [U], U.ap[:, k, l0:l0 + 128], [WG], WG.ap[:, k, 0:512], k == 0, k == 7)
                    st[("z", cc)] = ps

                def silu_z(cc):
                    ps = st.pop(("z", cc))
                    act(bb.zs, bb.zs.ap[:, cc, :], [ps], ps.ap, AF.Silu)

                def part1():
                    mm_a(0)
                    yield
                    for j in range(6):
                        if j + 1 < 6:
                            mm_a(j + 1)
                        evac_a(j)
                        yield

                def part2():
                    seq = [("c", j) for j in range(6)] + [("z", cc) for cc in range(4)]
                    def issue(u):
                        (conv if u[0] == "c" else mm_z)(u[1])
                    def finish(u):
                        (silu_c if u[0] == "c" else silu_z)(u[1])
                    issue(seq[0])
                    for i, u in enumerate(seq):
                        if i + 1 < len(seq):
                            issue(seq[i + 1])
                        finish(u)
                return part1(), part2

            def stage_a(tb, cc, cb):
                bb = blk[tb % 2]
                c = tb * 4 + cc
                cs = slice(cc * 128, (cc + 1) * 128)
                sm = cb.small
                acum, cdarg, cd, ea = (sm.ap[:, i, :] for i in range(4))
                dt_g = dt_all.ap[:, c, 8 * g:8 * g + 8]
                hs = slice(8 * g, 8 * g + 8)
                psXb = psX.ap.bitcast(BF16)
                for j in range(4):
                    tr(psX, psXb[:, j * 128:(j + 1) * 128], bb.xsT, bb.xsT.ap[:, j, cs], ident_b, ident_b.ap)
                psMb = psM.ap.bitcast(BF16)
                tr(psM, psMb[:, 0:128], bb.BT, bb.BT.ap[:, cs], ident_b, ident_b.ap)
                mm(psM, psM.ap[:, 128:136], [tri_b], tri_b.ap, [a_hi], a_hi.ap[:, c, hs], True, False)
                mm(psM, psM.ap[:, 128:136], [tri_b], tri_b.ap, [a_lo], a_lo.ap[:, c, hs], False, True)
                mm(psM, psM.ap[:, 256:384], [bb.BT], bb.BT.ap[:, cs], [bb.CT], bb.CT.ap[:, cs], True, True)
                yield
                for half, pr in ((0, psR0), (1, psR1)):
                    h4 = slice(8 * g + 4 * half, 8 * g + 4 * half + 4)
                    pr3 = pr.ap.rearrange("p (a b) -> p a b", a=4)
                    mm(pr, pr3, [ident_b], ident_b.ap, [negm_b], bcast(negm_b.ap, [128, 4, 128], 1), True, False)
                    mm(pr, pr3, [tri_b], tri_b.ap, [na_hi], bcast(na_hi.ap[:, c, h4], [128, 4, 128], 2), False, False)
                    mm(pr, pr3, [tri_b], tri_b.ap, [na_lo], bcast(na_lo.ap[:, c, h4], [128, 4, 128], 2), False, False)
                    for q in range(4):
                        hh = 8 * g + 4 * half + q
                        o = pr.ap[:, q * 128:(q + 1) * 128]
                        mm(pr, o, [a_hi], a_hi.ap[:, c, hh:hh + 1].broadcast_to([128, 128]), [tri_b], tri_b.ap, False, False)
                        mm(pr, o, [a_lo], a_lo.ap[:, c, hh:hh + 1].broadcast_to([128, 128]), [tri_b], tri_b.ap, False, q == 3)
                    yield
                cp(cb.xs_tok, cb.xs_tok.ap, [psX], psXb[:, 0:512], eng="act")
                tt(cb.Xdt, v3(cb.Xdt.ap), [psX, dt_all], v3(psXb[:, 0:512]), bcast(dt_g, [128, 8, 64], 2), ALU.mult)
                cp(sm, acum, [psM], psM.ap[:, 128:136])
                cp(cb.Btok, cb.Btok.ap, [psM], psMb[:, 0:128], eng="act")
                yield
                Ev = Et.ap.rearrange("p a b -> p (a b)")
                act(Et, Ev[:, 0:512], [psR0], psR0.ap, AF.Exp)
                yield
                act(Et, Ev[:, 512:1024], [psR1], psR1.ap, AF.Exp)
                yield
                tt(sm, cdarg[:, 0:4], [psR0, sm], psR0.ap[:, 127::128], acum[:, 0:4], ALU.add)
                tt(sm, cdarg[:, 4:8], [psR1, sm], psR1.ap[:, 127::128], acum[:, 4:8], ALU.add)
                yield
                act(sm, cd, [sm], cdarg, AF.Exp)
                act(sm, ea, [sm], acum, AF.Exp)
                yield

            def stage_a2(tb, cc, cb):
                sm = cb.small
                tt(cb.MT, cb.MT.ap, [Et, psM], Et.ap, bcast(psM.ap[:, 256:384], [128, 8, 128], 1), ALU.mult)
                tt(cb.Xd, v3(cb.Xd.ap), [cb.Xdt, Et], v3(cb.Xdt.ap), bcast(Et.ap[:, :, 127], [128, 8, 64], 2),
                   ALU.mult, eng="pool")
                tt(cb.dx, v3(cb.dx.ap), [cb.xs_tok, hv], v3(cb.xs_tok.ap),
                   bcast(hv.ap[:, 2, 8 * g:8 * g + 8], [128, 8, 64], 2), ALU.mult, eng="pool")

            def stage_b(tb, cc, cb):
                bb = blk[tb % 2]
                cs = slice(cc * 128, (cc + 1) * 128)
                l0 = tb * 512 + cc * 128
                sm = cb.small
                acum, cdarg, cd, ea = (sm.ap[:, i, :] for i in range(4))
                ss, rs = sm.ap[:, 6, 0:1], sm.ap[:, 7, 0:1]
                mm(psYo, psYo.ap, [bb.CT], bb.CT.ap[:, cs], [prevTb], prevTb.ap, True, True)
                Xdt3 = v3(cb.Xdt.ap)
                for hh in range(8):
                    mm(psY, psY.ap[:, hh * 64:(hh + 1) * 64], [cb.MT], cb.MT.ap[:, hh, :], [cb.Xdt], Xdt3[:, hh, :],
                       True, True)
                yield
                tt(cb.yoff, v3(cb.yoff.ap), [psYo, sm], v3(psYo.ap), bcast(ea, [128, 8, 64], 2), ALU.mult)
                yield
                mm(psYo, psYo.ap, [cb.Btok], cb.Btok.ap, [cb.Xd], cb.Xd.ap, True, True)
                tt(prevT, v3(prevT.ap), [prevT, sm], v3(prevT.ap), bcast(cd, [128, 8, 64], 2), ALU.mult)
                tt(cb.y, cb.y.ap, [psY, cb.yoff], psY.ap, cb.yoff.ap, ALU.add)
                yield
                tt(prevT, prevT.ap, [prevT, psYo], prevT.ap, psYo.ap, ALU.add)
                tt(cb.y, cb.y.ap, [cb.y, cb.dx], cb.y.ap, cb.dx.ap, ALU.add)
                yield
                cp(prevTb, prevTb.ap, [prevT], prevT.ap, eng="act")
                tt(cb.y, cb.y.ap, [cb.y, bb.zs], cb.y.ap, bb.zs.ap[:, cc, :], ALU.mult)
                yield
                act(cb.yn, cb.yn.ap, [cb.y], cb.y.ap, AF.Square, accum=ss, extra_w=[sm])
                act(sm, rs, [sm, epsb], ss, AF.Ln, bias=epsb.ap, scale=1.0 / 512)
                act(sm, rs, [sm], rs, AF.Exp, scale=-0.5)
                yield
                stt(cb.yn, cb.yn.ap, [cb.y, sm, normg], cb.y.ap, rs, normg.ap, ALU.mult, ALU.mult)
                yield
                psTb = psT.ap.bitcast(BF16)
                for j in range(4):
                    tr(psT, psTb[:, j * 128:(j + 1) * 128], cb.yn, cb.yn.ap[:, j * 128:(j + 1) * 128], ident_b, ident_b.ap)
                yield
                cp(cb.ynT, cb.ynT.ap, [psT], psTb[:, 0:512].rearrange("p (a b) -> p a b", a=4), eng="act")
                dma("sp", ynT_t, ynT_v[:, 4 * g:4 * g + 4, l0:l0 + 128], cb.ynT, cb.ynT.ap)
                yield

            return group_setup, group_begin, inproj, stage_a, stage_a2, stage_b

        gf = [group_fns(g) for g in range(4)]
        gf[0][0]()
        def zipn(*gens):
            done = object()
            alive = [True] * len(gens)
            while any(alive):
                for i, g_ in enumerate(gens):
                    if alive[i]:
                        alive[i] = next(g_, done) is not done

        def run(gen):
            for _ in gen:
                pass

        def dec(k):
            return k // 16, (k // 4) % 4, k % 4

        def stA(k):
            g_, tb_, cc_ = dec(k)
            return gf[g_][3](tb_, cc_, cbs[k % 2])

        def stA2(k):
            g_, tb_, cc_ = dec(k)
            return gf[g_][4](tb_, cc_, cbs[k % 2])

        def stB(k):
            g_, tb_, cc_ = dec(k)
            return gf[g_][5](tb_, cc_, cbs[k % 2])

        gf[0][0]()
        _p1, _p2 = gf[0][2](0)
        run(_p1)
        _p2()
        gf[0][1]()
        run(stA(0))
        stA2(0)
        for k in range(1, 64):
            g, tb, cc = dec(k)
            bi = k // 4
            if tb == 3 and cc == 0 and g < 3:
                gf[g + 1][0]()
            gens = [stA(k), stB(k - 1)]
            p2 = None
            if cc == 2 and bi + 1 < 16:
                g2, tb2 = divmod(bi + 1, 4)
                p1, p2 = gf[g2][2](tb2)
                gens.append(p1)
            zipn(*gens)
            if k % 16 == 0:
                gf[g][1]()
            stA2(k)
            if p2 is not None:
                p2()
        run(stB(63))
        release(mS)
        if stop_after == "ssd":
            release(mU)
            return

        mA = mark()
        wh_rot = Rot([alloc([3, 8, 128], BF16, "wh%d" % i) for i in range(2)])
        qk_rot = Rot([(alloc([L], BF16, "qT%d" % i), alloc([L], BF16, "kz0_%d" % i), alloc([L], BF16, "kz1_%d" % i))
                      for i in range(2)])
        for (_q, _k0, _k1) in qk_rot.items:
            memset(_k0, _k0.ap[64:128, :], 0.0)
            memset(_k1, _k1.ap[0:64, :], 0.0)
        vT = alloc([L], BF16, "vT")
        vtok_rot = Rot([alloc([16, 128], BF16, "vtok%d" % i) for i in range(2)])
        ET_rot = Rot([alloc([512], BF16, "ET%d" % i) for i in range(4)])
        nrm = [alloc([512], F32, "nrm%d" % i) for i in range(4)]
        sqa = alloc([512], BF16, "sqa")
        ao_rot = Rot([alloc([512], BF16, "ao%d" % i) for i in range(2)])
        g_rot = Rot([PS[0], PS[1], PS[2], PS[3]])
        psO = [PS[4], PS[5]]
        psZ = [PS[6], PS[7]]
        ev = Rot(["act", "dve"])
        for h in range(8):
            wh = wh_rot.next()
            dma("pool", wh, wh.ap, None, whead_d[h])
            qT, kz0, kz1 = qk_rot.next()
            kz = (kz0, kz1)
            for i, dst in enumerate((qT, None, vT)):
                for tb in range(4):
                    ps = g_rot.next()
                    cs = slice(tb * 512, (tb + 1) * 512)
                    for k in range(8):
                        mm(ps, ps.ap, [wh], wh.ap[:, i, k, :], [U], U.ap[:, k, cs], k == 0, k == 7)
                    if dst is None:
                        cp(kz0, kz0.ap[0:64, cs], [ps], ps.ap[0:64, :], eng="act")
                        cp(kz1, kz1.ap[64:128, cs], [ps], ps.ap[64:128, :], eng="dve")
                    else:
                        cp(dst, dst.ap[:, cs], [ps], ps.ap, eng=ev.next())
            vtok = vtok_rot.next()
            for t4 in range(4):
                ps = g_rot.next()
                psb = ps.ap.bitcast(BF16)
                for j in range(4):
                    tcn = t4 * 4 + j
                    tr(ps, psb[:, j * 128:(j + 1) * 128], vT, vT.ap[:, tcn * 128:(tcn + 1) * 128], ident_b, ident_b.ap)
                cp(vtok, vtok.ap[:, t4 * 4:(t4 + 1) * 4, :], [ps], psb[:, 0:512].rearrange("p (a b) -> p a b", a=4),
                   eng=ev.next())
            wide = h >= 2
            tasks = [(I, m, j) for I in range(4) for m in range(2) for j in range(4 * (I + 1))]
            tstate = {}

            def st_score(t):
                I, m, j = t
                q0 = 512 * I
                rows = slice(m * 64, (m + 1) * 64)
                diag = j >= 4 * I
                jj = j - 4 * I if diag else 0
                c0 = 128 * jj
                ps = g_rot.next()
                mm(ps, ps.ap[:, c0:512], [kz[m]], kz[m].ap[:, j * 128:(j + 1) * 128],
                   [qT], qT.ap[:, q0 + c0:q0 + 512], True, not diag)
                if diag:
                    mm(ps, ps.ap[:, c0:c0 + 128], [ident_b], ident_b.ap, [negm_b], negm_b.ap, False, True)
                tstate[t] = ps

            def st_pv(t):
                I, m, j = t
                q0 = 512 * I
                nkb = 4 * (I + 1)
                diag = j >= 4 * I
                jj = j - 4 * I if diag else 0
                c0 = 128 * jj
                ps = tstate.pop(t)
                ET = ET_rot.next()
                if wide:
                    col = h * 16 + (4 * I - j + 3)
                    act(ET, ET.ap[:, c0:512], [ps, ali2], ps.ap[:, c0:512], AF.Exp,
                        bias=ali2.ap[:, col:col + 1], scale=0.125)
                elif h == 1:
                    for i2 in range(jj // 2, 2):
                        lo = max(jj, 2 * i2)
                        r = 4 * I + (2 * i2 + 1) - j
                        col = h * 16 + r
                        act(ET, ET.ap[:, lo * 128:(2 * i2 + 2) * 128], [ps, alia], ps.ap[:, lo * 128:(2 * i2 + 2) * 128],
                            AF.Exp, bias=alia.ap[:, col:col + 1], scale=0.125)
                else:
                    for i in range(jj, 4):
                        r = 4 * I + i - j
                        col = h * 16 + r
                        act(ET, ET.ap[:, i * 128:(i + 1) * 128], [ps, alia], ps.ap[:, i * 128:(i + 1) * 128],
                            AF.Exp, bias=alia.ap[:, col:col + 1], scale=0.125)
                mm(psO[m], psO[m].ap[:, c0:512], [vtok], vtok.ap[:, j, :], [ET], ET.ap[:, c0:512],
                   j == 0, j == nkb - 1)
                mm(psZ[m], psZ[m].ap[:, c0:512], [ones_b], ones_b.ap, [ET], ET.ap[:, c0:512],
                   j == 0, j == nkb - 1)
                if m == 1 and j == nkb - 1:
                    r0, r1, t0_, t1_ = nrm
                    cp(r0, r0.ap, [psZ[0]], psZ[0].ap, eng="act")
                    cp(r1, r1.ap, [psZ[1]], psZ[1].ap, eng="act")
                    cp(t0_, t0_.ap, [psO[0]], psO[0].ap, eng="act")
                    cp(t1_, t1_.ap, [psO[1]], psO[1].ap, eng="act")

                    def s1():
                        recip(r0, r0.ap, [r0], r0.ap)
                    def s2():
                        recip(r1, r1.ap, [r1], r1.ap)
                    def s3():
                        tt(t0_, t0_.ap, [t0_, r0], t0_.ap, r0.ap, ALU.mult)
                        tt(t1_, t1_.ap, [t1_, r1], t1_.ap, r1.ap, ALU.mult)
                    def s4():
                        stt(t0_, t0_.ap, [t1_, nlam, t0_], t1_.ap, nlam.ap[:, 0:1], t0_.ap, ALU.mult, ALU.add)
                    def s5():
                        act(sqa, sqa.ap, [t0_], t0_.ap, AF.Square)
                    def s6():
                        psn = g_rot.next()
                        mm(psn, psn.ap, [ones_b], ones_b.ap, [sqa], sqa.ap, True, True)
                        act(r0, r0.ap, [psn, epsb], psn.ap, AF.Ln, bias=epsb.ap, scale=1.0 / 128)
                    def s7():
                        act(r0, r0.ap, [r0], r0.ap, AF.Exp, scale=-0.5)
                    def s8(q0=q0):
                        ao = ao_rot.next()
                        stt(ao, ao.ap, [t0_, sg08, r0], t0_.ap, sg08.ap[:, 0:1], r0.ap, ALU.mult, ALU.mult)
                        dma("sp", aoT_t, aoT_d[h * 128:(h + 1) * 128, q0:q0 + 512], ao, ao.ap)
                    pending.extend([s1, s2, s3, s4, s5, s6, s7, s8])

            pending = []
            LOOK = 2
            for idx in range(len(tasks) + LOOK):
                if idx < len(tasks):
                    st_score(tasks[idx])
                if idx >= LOOK:
                    st_pv(tasks[idx - LOOK])
                    if pending and tasks[idx - LOOK][2] >= 1:
                        pending.pop(0)()
            while pending:
                pending.pop(0)()
        release(mA)
        if stop_after == "attn":
            release(mU)
            return

        mT = mark()
        YNO = alloc([8192], F32, "YNO")
        YN = YNO.ap.bitcast(BF16).rearrange("p (a b) -> p a b", a=16)
        O = YNO.ap.rearrange("p (a b) -> p a b", a=8)
        AO = alloc([8, 1024], BF16, "AO")
        MG = alloc([8, 1024], BF16, "MG")
        wt_rot = Rot([alloc([40, 128], BF16, "wt%d" % i) for i in range(2)])
        wo_rot = Rot([alloc([8, 128], BF16, "wo%d" % i) for i in range(2)])
        gg = [alloc([512], F32, "gg%d" % i) for i in range(4)]
        rstd_rot = Rot([alloc([512], F32, "rstd%d" % i) for i in range(2)])
        sq_rot = Rot([alloc([512], BF16, "sq%d" % i) for i in range(2)])
        tmp_rot = Rot([gg[2], gg[3]])
        set_rot = Rot([PS[0:4], PS[4:8]])
        one_rot = Rot(PS)
        ynT_v2 = ynT_d.rearrange("(c p) t -> p c t", p=128)
        aoT_v2 = aoT_d.rearrange("(c p) t -> p c t", p=128)
        for half in range(2):
            hs = slice(half * 1024, (half + 1) * 1024)
            dma("sp", YNO, YN, ynT_t, ynT_v2[:, :, hs])
            dma("sp", AO, AO.ap, aoT_t, aoT_v2[:, :, hs])
            for d in range(8):
                wt = wt_rot.next()
                dma("pool", wt, wt.ap, None, wtail_d[d])
                for tb in range(2):
                    cs = slice(tb * 512, (tb + 1) * 512)
                    t0 = half * 1024 + tb * 512
                    pg0, pg1, pys, pya = set_rot.next()
                    for k in range(8):
                        mm(pg0, pg0.ap, [wt], wt.ap[:, k, :], [U], U.ap[:, k, t0:t0 + 512], k == 0, k == 7)
                    for k in range(8):
                        mm(pg1, pg1.ap, [wt], wt.ap[:, 8 + k, :], [U], U.ap[:, k, t0:t0 + 512], k == 0, k == 7)
                    for c in range(16):
                        mm(pys, pys.ap, [wt], wt.ap[:, 16 + c, :], [YNO], YN[:, c, cs], c == 0, c == 15)
                    for c in range(8):
                        mm(pya, pya.ap, [wt], wt.ap[:, 32 + c, :], [AO], AO.ap[:, c, cs], c == 0, c == 7)
                    i0 = (tb % 2) * 2
                    g0, g1 = gg[i0], gg[i0 + 1]
                    act(g0, g0.ap, [pg0, gateb], pg0.ap, AF.Sigmoid, bias=gateb.ap[:, d:d + 1], scale=1.0)
                    act(g1, g1.ap, [pg1, gateb], pg1.ap, AF.Sigmoid, bias=gateb.ap[:, 8 + d:9 + d], scale=1.0)
                    tt(g0, g0.ap, [g0, pys], g0.ap, pys.ap, ALU.mult)
                    tt(g1, g1.ap, [g1, pya], g1.ap, pya.ap, ALU.mult)
                    tt(MG, MG.ap[:, d, cs], [g0, g1], g0.ap, g1.ap, ALU.add)
            for d in range(8):
                wo = wo_rot.next()
                dma("pool", wo, wo.ap, None, wout_d[d])
                for tb in range(2):
                    cs = slice(tb * 512, (tb + 1) * 512)
                    ps = one_rot.next()
                    for k in range(8):
                        mm(ps, ps.ap, [wo], wo.ap[:, k, :], [MG], MG.ap[:, k, cs], k == 0, k == 7)
                    cp(YNO, O[:, d, cs], [ps], ps.ap, eng="act")
            for tb in range(2):
                t0 = half * 1024 + tb * 512
                cs = slice(tb * 512, (tb + 1) * 512)
                rstd = rstd_rot.next()
                rms_stats([YNO], [O[:, c, cs] for c in range(8)], D, one_rot.next(), sq_rot, rstd, rstd.ap)
                for c in range(8):
                    tmp = tmp_rot.next()
                    stt(tmp, tmp.ap, [YNO, gains, rstd], O[:, c, cs], gains.ap[:, 3, c:c + 1], rstd.ap,
                        ALU.mult, ALU.mult)
                    tt(H, H.ap[:, c, t0:t0 + 512], [H, tmp], H.ap[:, c, t0:t0 + 512], tmp.ap, ALU.add,
                       eng=("pool" if c % 3 == 2 else "dve"))
        release(mT)
        release(mU)

    if stop_after != "ffn1":
        mixer()
        dump_H("h2T")
    outT_v = outT_d.rearrange("(c p) t -> p c t", p=128)

    def store_half0():
        t_ = T(outT_d, "outT_h0")
        out_ops.append(dma("sp", t_, outT_v[:, :, 0:1024], H, H.ap[:, :, 0:1024]))

    if stop_after is None:
        ffn("ffn2", 4, 5, out_hook=store_half0)
        lo = 1024
    else:
        lo = 0

    for c in range(8):
        t_ = T(outT_d, "outT%d" % c)
        out_ops.append(dma("sp", t_, outT_d[c * 128:(c + 1) * 128, lo:L], H, H.ap[:, c, lo:L]))
    S.add("sp", lambda e: None, extra=out_ops + [op for op in S.dma_ops[-N_DMA_SEMS:]])
    S.emit()
    return nc, S


def _colblk(W):
    K, N = W.shape
    return np.ascontiguousarray(W.reshape(K // 128, 128, N // 128, 128).transpose(2, 1, 0, 3))


def _rowmaj(W):
    K, N = W.shape
    return np.ascontiguousarray(W.reshape(K // 128, 128, N).transpose(1, 0, 2))


def _fm(v):
    return np.ascontiguousarray(v.reshape(-1, 128).T)


def _bc(v):
    return np.ascontiguousarray(np.broadcast_to(v[None, :], (128, v.shape[0])))


def make_shared(inp):
    f = lambda a: np.asarray(a, dtype=np.float32)
    sh = {}
    for pre in ("ffn1", "ffn2"):
        sh[pre + "_wg"] = _colblk(f(inp[pre + "_w_gate"])[0])
        sh[pre + "_wu"] = _colblk(f(inp[pre + "_w_up"])[0])
        sh[pre + "_wd"] = _colblk(f(inp[pre + "_w_down"])[0])
    w_in = f(inp["w_in"])[0]
    z = w_in[:, 0:2048]
    xbc = w_in[:, 2048:5120]
    dtw = w_in[:, 5120:5152]
    q = w_in[:, 5152:6176]
    k = w_in[:, 6176:7200]
    v = w_in[:, 7200:8224]
    gate = w_in[:, 8224:10272]
    grp = []
    for g in range(4):
        cols = np.concatenate([z[:, g * 512:(g + 1) * 512], xbc[:, g * 512:(g + 1) * 512],
                               xbc[:, 2048 + g * 128:2048 + (g + 1) * 128],
                               xbc[:, 2560 + g * 128:2560 + (g + 1) * 128]], axis=1)
        grp.append(_rowmaj(cols))
    sh["w_grp"] = np.stack(grp)
    sh["w_dt"] = _rowmaj(dtw)
    qb, kb, vb = _colblk(q), _colblk(k), _colblk(v)
    sh["w_head"] = np.ascontiguousarray(np.stack([qb, kb, vb], axis=2))
    gb = _colblk(gate)
    wbs = _colblk(f(inp["ssd_w_branch"])[0])
    wbd = _colblk(f(inp["da_w_branch"])[0])
    sh["w_tail"] = np.ascontiguousarray(np.concatenate([gb[0:8], gb[8:16], wbs, wbd], axis=2))
    sh["w_outb"] = _colblk(f(inp["w_out"])[0])
    gl = [inp[n] for n in ("ffn1_pre_g", "ffn1_post_g", "mix_pre_g", "mix_post_g", "ffn2_pre_g", "ffn2_post_g")]
    sh["gains"] = np.ascontiguousarray(np.stack([_fm(f(g_)[0]) for g_ in gl], axis=1))
    sh["gate_b"] = _fm(f(inp["gate_b"])[0])
    cw = f(inp["ssd_conv_w"])[0]
    sh["conv_w"] = np.ascontiguousarray(cw.reshape(4, 24, 128).transpose(2, 1, 0))
    sh["conv_b"] = _fm(f(inp["ssd_conv_b"])[0])
    sh["headvec"] = np.ascontiguousarray(np.stack([_bc(f(inp["ssd_dt_bias"])[0]), _bc(f(inp["ssd_A_log"])[0]),
                                                   _bc(f(inp["ssd_D"])[0])], axis=1))
    sh["ssd_norm_g"] = _bc(f(inp["ssd_norm_g"])[0])
    sh["subln_g"] = np.ascontiguousarray(f(inp["da_subln_g"])[0].reshape(128, 1))
    sh["lamv"] = np.ascontiguousarray(np.stack([_bc(f(inp[n])[0]) for n in
                                                ("da_lambda_q1", "da_lambda_k1", "da_lambda_q2", "da_lambda_k2")], axis=1))
    p = np.arange(128, dtype=np.float32)
    ident = np.eye(128, dtype=np.float32)
    tri = (p[:, None] <= p[None, :]).astype(np.float32)
    negm = np.where(p[None, :] >= p[:, None], 0.0, NEG).astype(np.float32)
    slopes = 2.0 ** (-(np.arange(8, dtype=np.float32) + 1.0))
    r = np.arange(16, dtype=np.float32)
    ali = (slopes[None, :, None] * (p[:, None, None] - 128.0 * r[None, None, :] - 127.0)).reshape(128, 128)
    ali2 = (slopes[None, :, None] * (p[:, None, None] - 128.0 * (r[None, None, :] - 3.0) - 511.0)).reshape(128, 128)
    sh["consts"] = np.ascontiguousarray(np.stack([ident, tri, negm, ali.astype(np.float32)], axis=1))
    sh["alibi2"] = np.ascontiguousarray(ali2.astype(np.float32))
    return sh


_CACHE = {}


def kernel(**inputs):
    x = np.asarray(inputs["x"], dtype=np.float32)
    sh = make_shared(inputs)
    if "nc" not in _CACHE:
        _CACHE["nc"] = build_program()[0]
    nc = _CACHE["nc"]
    in_maps = []
    for b in range(8):
        m = dict(sh)
        m["xT"] = np.ascontiguousarray(x[b].T)
        in_maps.append(m)
    res = run_bass_kernel_spmd(nc, in_maps, core_ids=list(range(8)))
    out = np.stack([np.ascontiguousarray(res.results[b]["outT"].T) for b in range(8)], axis=0)
    return out.astype(np.float32)
```
